# Optimizing a Trainium2 kernel written in Bass

```python
import math
import jax, jax.numpy as jnp
from jax import lax
import numpy as np

D_MODEL = 1024
BATCH = 1
SEQ = 16384
DEPTH = 1

D_MIX = D_MODEL
GDN_HEADS = 4
GDN_DK = 128
GDN_DV = 128
GDN_CHUNK = 64
CONV_K = 4
A_LOG_MIN = 1.0
A_LOG_MAX = 8.0
DIFF_HEADS = 4
DIFF_D = 64
ROPE_THETA = 10000.0
Q_BLOCK = 128
LAMBDA_STD = 0.1
EPS = 1e-6
SUBLN_EPS = 1e-5

GDN_QK = GDN_HEADS * GDN_DK
GDN_V = GDN_HEADS * GDN_DV
DIFF_QK = DIFF_HEADS * 2 * DIFF_D
DIFF_V = DIFF_HEADS * 2 * DIFF_D
SPLIT_SIZES = (GDN_QK, GDN_QK, GDN_V, GDN_V, GDN_HEADS, GDN_HEADS,
               DIFF_QK, DIFF_QK, DIFF_V, DIFF_V)
D_IN_PROJ = sum(SPLIT_SIZES)
CONV_CH = 2 * GDN_QK + GDN_V

kernel_name = "hybrid_gdn_diffattn_parallel_heads"


def rms_norm(x, w, eps=EPS):
    x = x.astype(jnp.float32)
    return x * lax.rsqrt(jnp.mean(x * x, axis=-1, keepdims=True) + eps) * w.astype(jnp.float32)


def l2_norm(x, eps=EPS):
    return x * lax.rsqrt(jnp.sum(x * x, axis=-1, keepdims=True) + eps)


def rope_tables(seq, dim):
    inv_freq = ROPE_THETA ** (-jnp.arange(0, dim, 2, dtype=jnp.float32) / dim)
    ang = jnp.arange(seq, dtype=jnp.float32)[:, None] * inv_freq[None, :]
    return jnp.cos(ang), jnp.sin(ang)


def apply_rope(x, cos, sin):
    c = cos[None, :, None, None, :]
    s = sin[None, :, None, None, :]
    x1, x2 = jnp.split(x, 2, axis=-1)
    return jnp.concatenate([x1 * c - x2 * s, x2 * c + x1 * s], axis=-1)


def causal_depthwise_conv(x, w):
    c = x.shape[-1]
    return lax.conv_general_dilated(
        x, w[:, None, :], window_strides=(1,), padding=[(CONV_K - 1, 0)],
        dimension_numbers=("NWC", "WIO", "NWC"), feature_group_count=c)


def gated_delta_rule_chunked(q, k, v, g, beta):
    b, h, s, dk = q.shape
    dv = v.shape[-1]
    c = GDN_CHUNK
    n = s // c
    q = q.reshape(b, h, n, c, dk)
    k = k.reshape(b, h, n, c, dk)
    v = v.reshape(b, h, n, c, dv)
    beta = beta.reshape(b, h, n, c)
    g = jnp.cumsum(g.reshape(b, h, n, c), axis=-1)
    tril = jnp.tril(jnp.ones((c, c), dtype=bool))
    strict = jnp.tril(jnp.ones((c, c), dtype=bool), -1)
    decay = jnp.exp(jnp.where(tril, g[..., :, None] - g[..., None, :], -jnp.inf))
    k_beta = k * beta[..., None]
    v_beta = v * beta[..., None]
    lower = jnp.where(strict, jnp.einsum("bhncd,bhnmd->bhncm", k_beta, k) * decay, 0.0)
    a_mat = lower + jnp.eye(c, dtype=q.dtype)
    rhs = jnp.concatenate([v_beta, k_beta * jnp.exp(g)[..., None]], axis=-1)
    sol = lax.linalg.triangular_solve(a_mat, rhs, left_side=True, lower=True, unit_diagonal=True)
    u, w = sol[..., :dv], sol[..., dv:]
    qk_intra = jnp.where(tril, jnp.einsum("bhncd,bhnmd->bhncm", q, k) * decay, 0.0)

    def step(state, inp):
        q_c, k_c, u_c, w_c, g_c, a_c = inp
        v_new = u_c - jnp.einsum("bhcd,bhde->bhce", w_c, state)
        o = (jnp.einsum("bhcd,bhde->bhce", q_c * jnp.exp(g_c)[..., None], state)
             + jnp.einsum("bhcm,bhme->bhce", a_c, v_new))
        g_last = g_c[..., -1]
        k_dec = k_c * jnp.exp(g_last[..., None] - g_c)[..., None]
        state = state * jnp.exp(g_last)[..., None, None] + jnp.einsum("bhcd,bhce->bhde", k_dec, v_new)
        return state, o

    xs = tuple(jnp.moveaxis(t, 2, 0) for t in (q, k, u, w, g, qk_intra))
    state0 = jnp.zeros((b, h, dk, dv), dtype=q.dtype)
    _, o = lax.scan(step, state0, xs)
    return jnp.moveaxis(o, 0, 2).reshape(b, h, s, dv)


def diff_attention(q, k, v, lam):
    b, s, h, _, d = q.shape
    nblk = s // Q_BLOCK
    scale = 1.0 / math.sqrt(d)
    qb = jnp.moveaxis(q.reshape(b, nblk, Q_BLOCK, h, 2, d), 1, 0)
    key_pos = jnp.arange(s)

    def one_block(args):
        i, q_i = args
        sc = jnp.einsum("bqhcd,bkhcd->bhcqk", q_i, k) * scale
        q_pos = i * Q_BLOCK + jnp.arange(Q_BLOCK)
        mask = key_pos[None, :] <= q_pos[:, None]
        p = jax.nn.softmax(jnp.where(mask, sc, -jnp.inf), axis=-1)
        p = p[:, :, 0] - lam * p[:, :, 1]
        return jnp.einsum("bhqk,bkhe->bqhe", p, v)

    o = lax.map(one_block, (jnp.arange(nblk), qb))
    return jnp.moveaxis(o, 0, 1).reshape(b, s, h, 2 * d)


def hybrid_mixer(layer_idx, x, cos, sin, w_norm, w_in, conv_w, a_log, dt_bias, gdn_norm_w,
                 q_norm_w, k_norm_w, lambda_q1, lambda_k1, lambda_q2, lambda_k2, subln_w, w_out):
    b, s, _ = x.shape
    f32 = jnp.float32
    hn = rms_norm(x, w_norm)
    proj = jnp.einsum("bsd,de->bse", hn, w_in.astype(f32))
    idx = [int(t) for t in np.cumsum(SPLIT_SIZES)[:-1]]
    q_a, k_a, v_a, z_a, b_a, a_a, q_b, k_b, v_b, z_b = jnp.split(proj, idx, axis=-1)

    qkv = jax.nn.silu(causal_depthwise_conv(jnp.concatenate([q_a, k_a, v_a], -1), conv_w.astype(f32)))
    q_a, k_a, v_a = jnp.split(qkv, [GDN_QK, 2 * GDN_QK], axis=-1)
    to_heads = lambda t, dh: jnp.transpose(t.reshape(b, s, GDN_HEADS, dh), (0, 2, 1, 3))
    q_a = l2_norm(to_heads(q_a, GDN_DK)) * (GDN_DK ** -0.5)
    k_a = l2_norm(to_heads(k_a, GDN_DK))
    v_a = to_heads(v_a, GDN_DV)
    beta = jnp.transpose(jax.nn.sigmoid(b_a), (0, 2, 1))
    g = -jnp.exp(a_log.astype(f32)) * jax.nn.softplus(a_a + dt_bias.astype(f32))
    g = jnp.transpose(g, (0, 2, 1))
    o_a = gated_delta_rule_chunked(q_a, k_a, v_a, g, beta)
    o_a = jnp.transpose(o_a, (0, 2, 1, 3))
    o_a = rms_norm(o_a, gdn_norm_w) * jax.nn.silu(z_a.reshape(b, s, GDN_HEADS, GDN_DV))
    o_a = o_a.reshape(b, s, GDN_V)

    q_b = apply_rope(rms_norm(q_b.reshape(b, s, DIFF_HEADS, 2, DIFF_D), q_norm_w), cos, sin)
    k_b = apply_rope(rms_norm(k_b.reshape(b, s, DIFF_HEADS, 2, DIFF_D), k_norm_w), cos, sin)
    v_b = v_b.reshape(b, s, DIFF_HEADS, 2 * DIFF_D)
    lam_init = 0.8 - 0.6 * math.exp(-0.3 * layer_idx)
    lam = (jnp.exp(jnp.sum(lambda_q1.astype(f32) * lambda_k1.astype(f32)))
           - jnp.exp(jnp.sum(lambda_q2.astype(f32) * lambda_k2.astype(f32))) + lam_init)
    o_b = diff_attention(q_b, k_b, v_b, lam)
    o_b = rms_norm(o_b, subln_w, SUBLN_EPS) * (1.0 - lam_init)
    o_b = (o_b * jax.nn.silu(z_b.reshape(b, s, DIFF_HEADS, 2 * DIFF_D))).reshape(b, s, DIFF_V)

    o = jnp.concatenate([o_a, o_b], axis=-1).astype(x.dtype)
    return jnp.einsum("bse,ed->bsd", o, w_out)


def setup_inputs(seed: int = 0) -> dict:
    key = jax.random.key(seed)
    ks = jax.random.split(key, 16)
    nrm = lambda k, shape, std: jax.random.normal(k, shape, jnp.float32) * std
    return {
        "x": jax.random.normal(ks[0], (BATCH, SEQ, D_MODEL), jnp.float32),
        "w_norm": 1.0 + nrm(ks[1], (DEPTH, D_MODEL), 0.02),
        "w_in": nrm(ks[2], (DEPTH, D_MODEL, D_IN_PROJ), D_MODEL ** -0.5),
        "conv_w": nrm(ks[3], (DEPTH, CONV_K, CONV_CH), CONV_K ** -0.5),
        "a_log": jnp.log(jax.random.uniform(ks[4], (DEPTH, GDN_HEADS), jnp.float32, A_LOG_MIN, A_LOG_MAX)),
        "dt_bias": nrm(ks[5], (DEPTH, GDN_HEADS), 0.1),
        "gdn_norm_w": 1.0 + nrm(ks[6], (DEPTH, GDN_DV), 0.02),
        "q_norm_w": 1.0 + nrm(ks[7], (DEPTH, DIFF_D), 0.02),
        "k_norm_w": 1.0 + nrm(ks[8], (DEPTH, DIFF_D), 0.02),
        "lambda_q1": nrm(ks[9], (DEPTH, DIFF_D), LAMBDA_STD),
        "lambda_k1": nrm(ks[10], (DEPTH, DIFF_D), LAMBDA_STD),
        "lambda_q2": nrm(ks[11], (DEPTH, DIFF_D), LAMBDA_STD),
        "lambda_k2": nrm(ks[12], (DEPTH, DIFF_D), LAMBDA_STD),
        "subln_w": 1.0 + nrm(ks[13], (DEPTH, 2 * DIFF_D), 0.02),
        "w_out": nrm(ks[14], (DEPTH, D_MIX, D_MODEL), D_MIX ** -0.5),
    }


def reference(x, w_norm, w_in, conv_w, a_log, dt_bias, gdn_norm_w, q_norm_w, k_norm_w,
              lambda_q1, lambda_k1, lambda_q2, lambda_k2, subln_w, w_out):
    cos, sin = rope_tables(x.shape[1], DIFF_D)
    for l in range(DEPTH):
        x = x + hybrid_mixer(l, x, cos, sin, w_norm[l], w_in[l], conv_w[l], a_log[l], dt_bias[l],
                             gdn_norm_w[l], q_norm_w[l], k_norm_w[l], lambda_q1[l], lambda_k1[l],
                             lambda_q2[l], lambda_k2[l], subln_w[l], w_out[l])
    return x
```

```python
import math
from contextlib import ExitStack

import numpy as np
import ml_dtypes
import concourse.bass as bass
import concourse.mybir as mybir
from concourse.bass_utils import run_bass_kernel_spmd

F32 = mybir.dt.float32
BF16 = mybir.dt.bfloat16
AF = mybir.ActivationFunctionType
ALU = mybir.AluOpType
AX = mybir.AxisListType

D = 1024
NCH = 8
NH = 4
NCORE = 8
DIN = 4104
NEG = -30000.0
LAM_INIT = 0.8 - 0.6 * math.exp(-0.3 * 0)


class Sched:
    ENG = ("pe", "act", "dve", "pool", "sp")
    PSUM_KEYS = frozenset(("pA", "pB", "pS", "pT", "pG", "pN", "pM", "pQ"))
    XLAT = 0.5
    CP_PRIO = True

    def __init__(self, nc, es, reorder=True):
        self.nc, self.es, self.reorder = nc, es, reorder
        self.all = []
        self.last_w = {}
        self.readers = {}

    def sb(self, name, shape, dtype):
        return self.es.enter_context(self.nc.sbuf_tensor("s_" + name, list(shape), dtype))

    def ps(self, name, shape, dtype=F32):
        return self.es.enter_context(self.nc.psum_tensor("p_" + name, list(shape), dtype))

    def _add(self, eng, fn, reads, writes, dma_slot=None, cost=0.3, lat=None):
        op = dict(eng=eng, id=len(self.all), fn=fn, deps=set(), slot=dma_slot, cost=cost,
                  lat=(cost if lat is None else lat), succ=[])
        deps = op["deps"]
        for k in reads:
            w = self.last_w.get(k)
            if w is not None:
                deps.add(w["id"])
            if k in self.PSUM_KEYS:
                for r in self.readers.get(k, ()):
                    if r["eng"] != eng:
                        deps.add(r["id"])
        for k in writes:
            w = self.last_w.get(k)
            if w is not None:
                deps.add(w["id"])
            for r in self.readers.get(k, ()):
                deps.add(r["id"])
        for k in reads:
            self.readers.setdefault(k, []).append(op)
        for k in writes:
            self.last_w[k] = op
            self.readers[k] = []
        self.all.append(op)
        return op

    def op(self, eng, fn, reads=(), writes=(), cost=0.3):
        return self._add(eng, fn, list(reads), list(writes), cost=cost)

    def dma(self, out, in_, reads=(), writes=(), slot=None, eng="sp", nbytes=65536):
        return self._add(eng, lambda e: e.dma_start(out=out, in_=in_), list(reads), list(writes), dma_slot=slot,
                         cost=0.15, lat=2.5 + nbytes / 150e3)

    def _schedule(self):
        import heapq
        ops = self.all
        if not self.reorder:
            order = {e: [o for o in ops if o["eng"] == e] for e in self.ENG}
            return order
        indeg = [0] * len(ops)
        for o in ops:
            indeg[o["id"]] = len(o["deps"])
            for d in o["deps"]:
                ops[d]["succ"].append(o["id"])
        ready_t = [0.0] * len(ops)
        fin = [0.0] * len(ops)
        rank = [0.0] * len(ops)
        if self.CP_PRIO:
            for o in reversed(ops):
                best_s = 0.0
                for sidx in o["succ"]:
                    v = rank[sidx] + (self.XLAT if ops[sidx]["eng"] != o["eng"] else 0.02)
                    if v > best_s:
                        best_s = v
                rank[o["id"]] = o["lat"] + best_s
        cand = {e: [] for e in self.ENG}
        avail = {e: [] for e in self.ENG}
        free = {e: 0.0 for e in self.ENG}
        order = {e: [] for e in self.ENG}
        for o in ops:
            if indeg[o["id"]] == 0:
                heapq.heappush(cand[o["eng"]], (0.0, o["id"]))
        left = len(ops)
        while left:
            best, be = None, None
            for e in self.ENG:
                if avail[e]:
                    t = free[e]
                elif cand[e]:
                    t = max(free[e], cand[e][0][0])
                else:
                    continue
                if best is None or t < best:
                    best, be = t, e
            assert be is not None, "scheduler stuck (cyclic deps?)"
            e = be
            while cand[e] and cand[e][0][0] <= best:
                ci = heapq.heappop(cand[e])[1]
                heapq.heappush(avail[e], (-rank[ci], ci))
            i = heapq.heappop(avail[e])[1]
            o = ops[i]
            free[e] = best + o["cost"]
            fin[i] = best + o["lat"]
            order[e].append(o)
            left -= 1
            for sidx in o["succ"]:
                so = ops[sidx]
                rt = fin[i] + (self.XLAT if so["eng"] != e else 0.02)
                if rt > ready_t[sidx]:
                    ready_t[sidx] = rt
                indeg[sidx] -= 1
                if indeg[sidx] == 0:
                    heapq.heappush(cand[so["eng"]], (ready_t[sidx], sidx))
        self.model_us = max(free.values())
        return order

    def emit(self, final_keys=()):
        nc = self.nc
        self._add("sp", None, list(final_keys), [], cost=0.0)
        order = self._schedule()
        ops = self.all
        pos = {}
        for e in self.ENG:
            for n_, o in enumerate(order[e]):
                pos[o["id"]] = n_
                o["signal"] = False
        for o in ops:
            need = {}
            for d in o["deps"]:
                do = ops[d]
                if do["slot"] is not None:
                    need[("dma", d)] = d
                elif do["eng"] == "pe" and o["eng"] == "pe":
                    continue
                else:
                    cur = need.get(do["eng"])
                    if cur is None or pos[cur] < pos[d]:
                        need[do["eng"]] = d
            o["need"] = sorted(need.values(), key=lambda d: (ops[d]["eng"], pos[d]))
            for d in o["need"]:
                ops[d]["signal"] = True
        ROT = 3000
        slot_sem, slot_cnt = {}, {}
        nsem = ndsem = 0
        for e in self.ENG:
            cnt, cur = 0, None
            for op in order[e]:
                if op["slot"] is not None:
                    s = op["slot"]
                    if s not in slot_sem or slot_cnt[s] + 16 > ROT:
                        slot_sem[s] = self.es.enter_context(nc.semaphore("dsem_%d" % ndsem))
                        ndsem += 1
                        slot_cnt[s] = 0
                    slot_cnt[s] += 16
                    op["sig"] = (slot_sem[s], slot_cnt[s])
                elif op["signal"]:
                    if cur is None or cnt >= ROT:
                        cur = self.es.enter_context(nc.semaphore("sem_%s_%d" % (e, nsem)))
                        nsem += 1
                        cnt = 0
                    cnt += 1
                    op["sig"] = (cur, cnt)
        self.stats = {e: (len(order[e]), sum(1 for o in order[e] if o["signal"])) for e in self.ENG}
        self.stats["dma_sems"] = ndsem
        self.stats["engine_sems"] = nsem
        self.stats["model_us"] = getattr(self, "model_us", None)
        self.stats["busy_us"] = {e: round(sum(o["cost"] for o in order[e]), 1) for e in self.ENG}
        block = self.es.enter_context(nc.Block())
        handles = dict(pe="tensor", act="scalar", dve="vector", pool="gpsimd", sp="sync")

        def make(e):
            def body(engh):
                seen = {}
                for op in order[e]:
                    for d in op["need"]:
                        s, v = ops[d]["sig"]
                        if seen.get(id(s), 0) < v:
                            engh.wait_ge(s, v)
                            seen[id(s)] = v
                    if op["fn"] is None:
                        continue
                    ins = op["fn"](engh)
                    if op["slot"] is not None:
                        ins.then_inc(op["sig"][0], 16)
                    elif op["signal"]:
                        ins.then_inc(op["sig"][0], 1)
            return body

        for e in self.ENG:
            getattr(block, handles[e])(make(e))


def build(NBLK, dbg=None, stop=None, reorder=True):
    S_ = NBLK * 128
    NOWN = NBLK // NCORE
    SO = NOWN * 128
    NT4 = NBLK // 4
    nc = bass.Bass("TRN2", target_bir_lowering=False)
    es = ExitStack()
    dt = nc.dram_tensor
    xT = dt("xT", [D, S_], F32, kind="ExternalInput").ap()
    xTo = dt("xTo", [D, SO], F32, kind="ExternalInput").ap()
    xo = dt("xo", [SO, D], F32, kind="ExternalInput").ap()
    w_in = dt("w_in", [D, DIN], F32, kind="ExternalInput").ap()
    w_out = dt("w_out", [D, D], F32, kind="ExternalInput").ap()
    wnorm = dt("wnorm", [128, NCH], F32, kind="ExternalInput").ap()
    convw = dt("convw", [128, 12, 4], F32, kind="ExternalInput").ap()
    vecs = dt("vecs", [1, 8 + 128 * 2 + 64 * 6], F32, kind="ExternalInput").ap()
    cosd = dt("cosd", [S_, 32], F32, kind="ExternalInput").ap()
    sind = dt("sind", [S_, 32], F32, kind="ExternalInput").ap()
    coso = dt("coso", [SO, 32], F32, kind="ExternalInput").ap()
    sino = dt("sino", [SO, 32], F32, kind="ExternalInput").ap()
    cmask = dt("cmask", [128, 5, 128], F32, kind="ExternalInput").ap()
    amaskd = dt("amask", [128, 8, 128], F32, kind="ExternalInput").ap()
    seld = dt("sel", [128, 8], F32, kind="ExternalInput").ap()
    outd = dt("out", [SO, D], F32, kind="ExternalOutput").ap()
    oscr = dt("oscr", [128, NCH, SO], BF16, kind="Internal").ap()
    dbg_out = {}

    with es:
        S = Sched(nc, es, reorder)
        sb, ps = S.sb, S.ps

        def fsz(ap):
            n = 1
            for d in list(ap.shape)[1:]:
                n *= int(d)
            return n

        def mm(out, lhsT, rhs, r, w, start=True, stop=True):
            c = 0.064 + fsz(rhs) / 2400.0
            if rhs.dtype == F32:
                c *= 4
            S.op("pe", lambda e: e.matmul(out, lhsT=lhsT, rhs=rhs, start=start, stop=stop), r, w, cost=c)

        def tr(out, in_, ident, r, w):
            S.op("pe", lambda e: e.transpose(out=out, in_=in_, identity=ident), r, w, cost=0.12)

        def act(out, in_, func, r, w, scale=None, bias=None):
            kw = {}
            if scale is not None:
                kw["scale"] = scale
            if bias is not None:
                kw["bias"] = bias
            S.op("act", lambda e: e.activation(out=out, in_=in_, func=func, **kw), r, w, cost=0.22 + fsz(out) / 1400.0)

        def ecost(eng, out):
            n = fsz(out)
            return {"dve": 0.12 + n / 960.0, "pool": 0.35 + n / 500.0, "act": 0.22 + n / 1400.0}[eng]

        def tt(eng, out, in0, in1, op, r, w):
            S.op(eng, lambda e: e.tensor_tensor(out=out, in0=in0, in1=in1, op=op), r, w, cost=ecost(eng, out))

        def ts(eng, out, in0, s1, op0, r, w, s2=None, op1=None):
            if op1 is None:
                S.op(eng, lambda e: e.tensor_scalar(out=out, in0=in0, scalar1=s1, scalar2=None, op0=op0), r, w, cost=ecost(eng, out))
            else:
                S.op(eng, lambda e: e.tensor_scalar(out=out, in0=in0, scalar1=s1, scalar2=s2, op0=op0, op1=op1), r, w, cost=ecost(eng, out))

        def stt(out, in0, scalar, in1, op0, op1, r, w):
            S.op("dve", lambda e: e.scalar_tensor_tensor(out=out, in0=in0, scalar=scalar, in1=in1, op0=op0, op1=op1), r, w,
                 cost=0.12 + fsz(out) / 480.0)

        def cp(eng, out, in_, r, w):
            if eng == "act":
                S.op("act", lambda e: e.copy(out=out, in_=in_), r, w, cost=ecost("act", out))
            else:
                S.op(eng, lambda e: e.tensor_copy(out=out, in_=in_), r, w, cost=ecost(eng, out))

        def red(out, in_, op, r, w):
            S.op("dve", lambda e: e.tensor_reduce(out=out, in_=in_, axis=AX.X, op=op), r, w, cost=0.12 + fsz(in_) / 960.0)

        def mset(eng, ap, val, w):
            S.op(eng, lambda e: e.memset(ap, val), [], w, cost=ecost(eng, ap))

        def rsqrt_act(out, in_, r, w, tmp, tmpk, scale, bias_ap):
            act(tmp, in_, AF.Ln, list(r) + ["cst"], [tmpk], scale=scale, bias=bias_ap)
            act(out, tmp, AF.Exp, [tmpk], w, scale=-0.5)

        def dbg_dump(name, ap_sb, shape, rkeys, dtype=F32):
            if dbg is None or name not in dbg:
                return
            d = dt("dbg_" + name, list(shape), dtype, kind="ExternalOutput").ap()
            dbg_out[name] = d
            S.dma(d, ap_sb, reads=rkeys, writes=["dbgo_" + name], slot="dbg_" + name)

        cm = sb("cm", [128, 5, 128], F32)
        S.dma(cm[:], cmask, writes=["cm"], slot="c_cm")
        Uf, negm, strictm, idf, posm = cm[:, 0, :], cm[:, 1, :], cm[:, 2, :], cm[:, 3, :], cm[:, 4, :]
        idb = sb("idb", [128, 128], BF16)
        cp("dve", idb[:], idf, ["cm"], ["idb"])
        onesf = sb("onesf", [128, 128], F32)
        onesb = sb("onesb", [128, 128], BF16)
        mset("pool", onesf[:], 1.0, ["onesf"])
        mset("pool", onesb[:], 1.0, ["onesb"])
        cst = sb("cst", [128, 4], F32)
        for i_, v_ in enumerate((1e-6, 1e-5, 1.0, 0.0)):
            mset("dve", cst[:, i_:i_ + 1], v_, ["cst"])
        NV = 8 + 256 + 384
        vec = sb("vec", [128, NV], F32)
        S.dma(vec[:], vecs[0:1, :].broadcast_to([128, NV]), writes=["vec"], slot="c_vec")
        a_log_b, dtb_b = vec[:, 0:4], vec[:, 4:8]
        gnw_b, snw_b = vec[:, 8:136], vec[:, 136:264]
        qnw_b, knw_b = vec[:, 264:328], vec[:, 328:392]
        lq1, lk1, lq2, lk2 = vec[:, 392:456], vec[:, 456:520], vec[:, 520:584], vec[:, 584:648]
        wn = sb("wn", [128, NCH], F32)
        S.dma(wn[:], wnorm, writes=["wn"], slot="c_wn")
        cw = sb("cw", [128, 12, 4], F32)
        S.dma(cw[:], convw, writes=["cw"], slot="c_cw")
        sel = sb("sel", [128, 8], F32)
        S.dma(sel[:], seld, writes=["sel"], slot="c_sel")
        coso_t = sb("coso_t", [128, NOWN, 32], F32)
        sino_t = sb("sino_t", [128, NOWN, 32], F32)
        S.dma(coso_t[:], coso.rearrange("(i p) d -> p i d", p=128), writes=["coso"], slot="c_coso")
        S.dma(sino_t[:], sino.rearrange("(i p) d -> p i d", p=128), writes=["sino"], slot="c_sino")

        sm = sb("sm", [128, 16], F32)
        act(sm[:, 0:4], a_log_b, AF.Exp, ["vec"], ["sm_a"])
        ts("dve", sm[:, 0:4], sm[:, 0:4], -1.0, ALU.mult, ["sm_a"], ["sm_a"])
        ltmp = sb("ltmp", [128, 64], F32)
        tt("dve", ltmp[:], lq1, lk1, ALU.mult, ["vec"], ["ltmp"])
        red(sm[:, 8:9], ltmp[:], ALU.add, ["ltmp"], ["sm_l1"])
        tt("dve", ltmp[:], lq2, lk2, ALU.mult, ["vec", "sm_l1"], ["ltmp"])
        red(sm[:, 9:10], ltmp[:], ALU.add, ["ltmp"], ["sm_l2"])
        act(sm[:, 8:10], sm[:, 8:10], AF.Exp, ["sm_l1", "sm_l2"], ["sm_l12"])
        tt("dve", sm[:, 4:5], sm[:, 8:9], sm[:, 9:10], ALU.subtract, ["sm_l12"], ["sm_lam"])
        ts("dve", sm[:, 4:5], sm[:, 4:5], LAM_INIT, ALU.add, ["sm_lam"], ["sm_lam"])
        ts("dve", sm[:, 5:6], sm[:, 4:5], -1.0, ALU.mult, ["sm_lam"], ["sm_nlam"])
        S.op("dve", lambda e: e.tensor_reduce(out=sm[:, 10:11], in_=qnw_b, axis=AX.X, op=ALU.max, apply_absolute_value=True), ["vec"], ["sm_mq"])
        S.op("dve", lambda e: e.tensor_reduce(out=sm[:, 11:12], in_=knw_b, axis=AX.X, op=ALU.max, apply_absolute_value=True), ["vec"], ["sm_mk"])
        tt("dve", sm[:, 6:7], sm[:, 10:11], sm[:, 11:12], ALU.mult, ["sm_mq", "sm_mk"], ["sm_bnd"])
        ts("dve", sm[:, 6:7], sm[:, 6:7], -8.0, ALU.mult, ["sm_bnd"], ["sm_bnd"])
        nega = [sm[:, h:h + 1] for h in range(NH)]
        lam_ap, nlam_ap, nbnd_ap = sm[:, 4:5], sm[:, 5:6], sm[:, 6:7]
        qnw2 = sb("qnw2", [128, 2, 64], F32)
        knw2 = sb("knw2", [128, 2, 64], F32)
        for c_ in range(2):
            cp("pool", qnw2[:, c_, :], qnw_b, ["vec"], ["qnw2"])
            cp("pool", knw2[:, c_, :], knw_b, ["vec"], ["knw2"])

        pA = ps("pA", [128, 512]); pB = ps("pB", [128, 512]); pS = ps("pS", [128, 512])
        pT = ps("pT", [128, 1024], BF16); pG = ps("pG", [128, 512]); pN = ps("pN", [128, 512])
        pQ = ps("pQ", [128, 512]); pM = ps("pM", [128, 512])

        NCOL = 1026
        Wb = sb("Wb", [128, NCH, NCOL], BF16)
        Wst = [sb("Wst0", [128, NCH, 128], F32)] * 2
        KT = sb("KT", [128, S_], BF16)
        Vg = sb("Vg", [128, NBLK, 130], BF16)
        mset("pool", Vg[:, :, 128:130], 1.0, ["Vg_ones"])
        QT2 = sb("QT2", [128, NOWN, 256], BF16)
        mset("pool", QT2[:], 0.0, ["QT2z"])
        zas = sb("zas", [128, NOWN, 128], BF16)
        zbs = sb("zbs", [128, NOWN, 128], BF16)
        oacc = sb("oacc", [128, NOWN, 128], F32)
        xTt = [sb("xTt%d" % i, [128, NCH, 256], BF16) for i in range(2)]
        cst4 = [sb("cs4_%d" % i, [128, 2, 2, 32], F32) for i in range(2)]
        amb = sb("amb", [128, 8, 128], BF16)
        amf = sb("amf", [128, 8, 128], F32)
        S.dma(amf[:], amaskd, writes=["amf"], slot="c_am")
        cp("dve", amb[:], amf[:], ["amf"], ["amb"])
        Sf = sb("Sf", [128, 128], F32)
        Sb_ = sb("Sb", [128, 128], BF16)
        xc = [sb("xc%d" % i, [128, 3, 131], F32) for i in range(3)]

        NB = 2

        def dbuf(name, shape, dtype, n=None):
            n = NB if n is None else n
            return [sb("%s%d" % (name, i), shape, dtype) for i in range(n)]

        xb_ = dbuf("xb", [128, NCH, 128], BF16)
        rbc_ = dbuf("rbc", [128, 128], F32)
        lnt_ = dbuf("lnt", [128, 384], F32)
        sc_ = dbuf("sc", [128, 16], F32)
        ycv_ = dbuf("ycv", [128, 3, 128], F32)
        ys_ = dbuf("ys", [128, 3, 128], F32)
        y2_ = dbuf("y2", [128, 2, 128], BF16)
        rn_ = dbuf("rn", [128, 2, 128], F32)
        qnT_ = dbuf("qnT", [128, 128], BF16)
        knT_ = dbuf("knT", [128, 128], BF16)
        vT_ = dbuf("vT", [128, 128], BF16)
        ktok_ = dbuf("ktok", [128, 128], BF16)
        vtok_ = dbuf("vtok", [128, 128], BF16)
        kb_ = dbuf("kb", [128, 128], BF16)
        kbg_ = dbuf("kbg", [128, 128], BF16)
        kdec_ = dbuf("kdec", [128, 128], BF16)
        vb_ = dbuf("vb", [128, 128], BF16)
        kbT_ = dbuf("kbT", [128, 128], BF16)
        gU_ = dbuf("gU", [128, 128], F32)
        eR_ = dbuf("eR", [128, 128], F32)
        GT_ = dbuf("GT", [128, 128], F32)
        DT_ = dbuf("DT", [128, 128], F32)
        DTs_ = dbuf("DTs", [128, 128], F32)
        qgT_ = dbuf("qgT", [128, 128], BF16)
        AqkT_ = dbuf("AqkT", [128, 128], BF16)
        nA_ = dbuf("nA", [128, 128], BF16, 4)
        nB_ = dbuf("nB", [128, 128], BF16, 4)
        PT_ = dbuf("PT", [128, 128], BF16, 4)
        u_ = dbuf("u", [128, 128], F32)
        wT_ = dbuf("wT", [128, 128], BF16)
        vnew_ = dbuf("vnew", [128, 128], BF16)
        kq_ = dbuf("kq", [128, 128], F32)
        ksq_ = dbuf("ksq", [128, 128], F32)
        kn_ = dbuf("kn", [128, 128], F32)
        rt_ = dbuf("rt", [128, 4, 64], F32)
        kr_ = dbuf("kr", [128, 128], BF16)
        obn = dbuf("obn", [128, 128], F32, 2)
        obb = dbuf("obb", [128, 128], BF16, 2)
        oTs = dbuf("oTs", [128, 128], BF16, 2)
        Pt = dbuf("Pt", [128, 512], BF16, 3)
        t0 = dbuf("t0", [128, 128], F32, 2)
        ob = dbuf("ob", [128, 128], F32, 2)

        def load_weights(h):
            groups = [h * 128, 512 + h * 128, 1024 + h * 128, 2568 + h * 128, 3080 + h * 128,
                      None, 2056 + h * 128, 1536 + h * 128, 3592 + h * 128]
            dst = [0, 128, 256, 384, 512, 640, 642, 770, 898]
            w_v = w_in.rearrange("(c p) n -> p c n", p=128)
            for gi, (c0, d0) in enumerate(zip(groups, dst)):
                st = Wst[gi % 2]
                sk = "Wst0"
                if c0 is None:
                    S.dma(st[:, :, 0:8], w_v[:, :, 2048:2056], writes=[sk], slot=sk)
                    for q_, src_c in enumerate((h, 4 + h)):
                        tt("dve", Wb[:, :, d0 + q_:d0 + q_ + 1], st[:, :, src_c:src_c + 1],
                           wn[:].unsqueeze(2), ALU.mult, [sk, "wn"], ["Wb"])
                    continue
                else:
                    S.dma(st[:, :, :], w_v[:, :, c0:c0 + 128], writes=[sk], slot=sk)
                    n = 128
                tt("dve" if gi % 2 == 0 else "pool", Wb[:, :, d0:d0 + n], st[:, :, 0:n],
                   wn[:].unsqueeze(2).broadcast_to([128, NCH, n]), ALU.mult, [sk, "wn"], ["Wb"])

        def rope_norm(src_ps, src_key, rstd_ap, rstd_key, w2, w2key, cos_ap, sin_ap, cskeys, p, outT_ap, outT_keys, pt_slot):
            kq, ksq, kn, rt, kr, sc, lnt = kq_[p], ksq_[p], kn_[p], rt_[p], kr_[p], sc_[p], lnt_[p]
            K = lambda n: "%s%d" % (n, p)
            act(kq[:], src_ps, AF.Identity, [src_key, rstd_key], [K("kq")], scale=rstd_ap)
            tt("pool", ksq[:], kq[:], kq[:], ALU.mult, [K("kq")], [K("ksq")])
            red(sc[:, 8:10], ksq[:].rearrange("p (c d) -> p c d", c=2), ALU.add, [K("ksq")], [K("sc_ms")])
            rsqrt_act(sc[:, 10:12], sc[:, 8:10], [K("sc_ms")], [K("sc_rn")], lnt[:, 0:2], K("lnt"), 1.0 / 64, cst[:, 0:1])
            tt("pool", kn[:].rearrange("p (c d) -> p c d", c=2), kq[:].rearrange("p (c d) -> p c d", c=2),
               sc[:, 10:12].unsqueeze(2).broadcast_to([128, 2, 64]), ALU.mult, [K("kq"), K("sc_rn")], [K("kn")])
            tt("pool", kn[:].rearrange("p (c d) -> p c d", c=2), kn[:].rearrange("p (c d) -> p c d", c=2), w2[:], ALU.mult,
               [K("kn"), w2key], [K("kn")])
            knv = kn[:].rearrange("p (c t d) -> p c t d", c=2, t=2)
            x1, x2 = knv[:, :, 0, :], knv[:, :, 1, :]
            cb = cos_ap.unsqueeze(1).broadcast_to([128, 2, 32])
            sbc = sin_ap.unsqueeze(1).broadcast_to([128, 2, 32])
            rtv = rt[:].rearrange("p f (c d) -> p f c d", c=2)
            tt("dve", rtv[:, 0], x1, cb, ALU.mult, [K("kn")] + cskeys, [K("rt0")])
            tt("pool", rtv[:, 1], x2, sbc, ALU.mult, [K("kn")] + cskeys, [K("rt1")])
            tt("dve", rtv[:, 2], x2, cb, ALU.mult, [K("kn")] + cskeys, [K("rt2")])
            tt("pool", rtv[:, 3], x1, sbc, ALU.mult, [K("kn")] + cskeys, [K("rt3")])
            krv = kr[:].rearrange("p (c t d) -> p c t d", c=2, t=2)
            tt("dve", krv[:, :, 0, :], rtv[:, 0], rtv[:, 1], ALU.subtract, [K("rt0"), K("rt1")], [K("kr_a")])
            tt("pool", krv[:, :, 1, :], rtv[:, 2], rtv[:, 3], ALU.add, [K("rt2"), K("rt3")], [K("kr_b")])
            tr(pT[:, pt_slot * 128:(pt_slot + 1) * 128], kr[:], idb[:], [K("kr_a"), K("kr_b"), "idb"], ["pT"])
            if isinstance(outT_ap, list):
                for (dst, r0, r1) in outT_ap:
                    cp("act", dst, pT[r0:r1, pt_slot * 128:(pt_slot + 1) * 128], ["pT", "QT2z"], outT_keys)
            else:
                cp("act", outT_ap, pT[:, pt_slot * 128:(pt_slot + 1) * 128], ["pT"], outT_keys)

        rstd_all = sb("rstd_all", [128, NBLK], F32)
        rstd_own = sb("rstd_own", [128, NOWN], F32)

        def x_block_prep(src_tile_ap, src_key, p, cache_ap, cache_key, first):
            xb, rbc, lnt, sc, dg = src_tile_ap, rbc_[p], lnt_[p], sc_[p], gU_[p]
            K = lambda n: "%s%d" % (n, p)
            if first:
                for c in range(NCH):
                    mm(pB[:, 384:512], xb[:, c, :], xb[:, c, :], [src_key], ["pB"], start=(c == 0), stop=(c == NCH - 1))
                tt("dve", lnt[:, 256:384], pB[:, 384:512], idf, ALU.mult, ["pB", "cm"], [K("lnt")])
                red(sc[:, 14:15], lnt[:, 256:384], ALU.add, [K("lnt")], [K("sc14")])
                rsqrt_act(cache_ap, sc[:, 14:15], [K("sc14")], [cache_key], sc[:, 15:16], K("sc15"), 1.0 / D, cst[:, 0:1])
            cp("pool", sc[:, 0:1], cache_ap, [cache_key], [K("sc_r")])
            act(dg[:], idf, AF.Identity, ["cm", cache_key], [K("gU")], scale=cache_ap)
            mm(pB[:, 384:512], onesf[:], dg[:], ["onesf", K("gU")], ["pB"])
            cp("act", rbc[:], pB[:, 384:512], ["pB"], [K("rbc")])

        for h in range(NH):
            if stop == "const":
                break
            load_weights(h)
            if stop == "weights":
                break
            xTo_v = xTo.rearrange("(c p) t -> p c t", p=128)
            for i in range(NOWN):
                p = i % 2
                K = lambda n: "%s%d" % (n, p)
                tl, tk = xTt[p], "xTt%d" % p
                S.dma(tl[:, :, 0:128], xTo_v[:, :, i * 128:(i + 1) * 128], writes=[tk], slot=tk, eng="pool")
                x_block_prep(tl[:, :, 0:128], tk, p, rstd_own[:, i:i + 1], "rstd_own%d" % i, h == 0)
                for c in range(NCH):
                    mm(pB[:, 0:384], tl[:, c, 0:128], Wb[:, c, 642:1026], [tk, "Wb"], ["pB"], start=(c == 0), stop=(c == NCH - 1))
                rope_norm(pB[:, 0:128], "pB", sc_[p][:, 0:1], K("sc_r"), qnw2, "qnw2", coso_t[:, i, :], sino_t[:, i, :],
                          ["coso", "sino"], p, [(QT2[0:64, i, 0:128], 0, 64), (QT2[64:128, i, 128:256], 64, 128)], ["QT%d" % i], 0)
                act(zas[:, i, :], pB[:, 128:256], AF.Silu, ["pB", K("sc_r")], ["zas%d" % i], scale=sc_[p][:, 0:1])
                act(zbs[:, i, :], pB[:, 256:384], AF.Silu, ["pB", K("sc_r")], ["zbs%d" % i], scale=sc_[p][:, 0:1])
            if stop == "own":
                break
            mset("pool", Sf[:], 0.0, ["Sf"])
            mset("pool", Sb_[:], 0.0, ["Sb"])
            mset("pool", xc[0][:, :, 0:3], 0.0, ["xch0"])
            xT_v = xT.rearrange("(c p) t -> p c t", p=128)
            for b in range(NBLK):
                p = b % NB
                K = lambda n: "%s%d" % (n, p)
                t4, tp = b // 2, (b // 2) % 2
                tl, tk = xTt[tp], "xTt%d" % tp
                cs4, csk = cst4[tp], "cs4_%d" % tp
                if b % 2 == 0:
                    S.dma(tl[:], xT_v[:, :, t4 * 256:(t4 + 1) * 256], writes=[tk], slot=tk, eng="pool")
                    S.dma(cs4[:, 0], cosd[t4 * 256:(t4 + 1) * 256, :].rearrange("(i p) d -> p i d", p=128), writes=[csk], slot=csk)
                    S.dma(cs4[:, 1], sind[t4 * 256:(t4 + 1) * 256, :].rearrange("(i p) d -> p i d", p=128), writes=[csk], slot=csk)
                bo = (b % 2) * 128
                x_block_prep(tl[:, :, bo:bo + 128], tk, p, rstd_all[:, b:b + 1], "rstd_all%d" % b, h == 0)
                xb, rbc, sc, lnt = tl[:, :, bo:bo + 128], rbc_[p], sc_[p], lnt_[p]
                for g in range(3):
                    for c in range(NCH):
                        mm(pA[:, g * 128:(g + 1) * 128], Wb[:, c, g * 128:(g + 1) * 128], xb[:, c, :], [tk, "Wb"], ["pA"],
                           start=(c == 0), stop=(c == NCH - 1))
                for c in range(NCH):
                    mm(pB[:, 0:258], xb[:, c, :], Wb[:, c, 384:642], [tk, "Wb"], ["pB"], start=(c == 0), stop=(c == NCH - 1))
                if stop == "s_proj":
                    continue
                rope_norm(pB[:, 0:128], "pB", sc[:, 0:1], K("sc_r"), knw2, "knw2", cs4[:, 0, b % 2, :], cs4[:, 1, b % 2, :],
                          [csk], p, KT[:, b * 128:(b + 1) * 128], ["KT%d" % b], 4)
                act(Vg[:, b, 0:128], pB[:, 128:256], AF.Identity, ["pB", K("sc_r")], ["Vg%d" % b], scale=sc[:, 0:1])
                pn = (b + 1) % NB
                xcp, xcn = xc[p], xc[pn]
                tt("dve", xcp[:, :, 3:131], pA[:, 0:384].rearrange("p (g t) -> p g t", g=3),
                   rbc[:].unsqueeze(1).broadcast_to([128, 3, 128]), ALU.mult, ["pA", K("rbc")], ["xcb%d" % p])
                cp("pool", xcn[:, :, 0:3], xcp[:, :, 128:131], ["xcb%d" % p], ["xch%d" % pn])
                if stop == "s_c1":
                    continue
                ycv, ys, y2, rn = ycv_[p], ys_[p], y2_[p], rn_[p]
                for g in range(3):
                    wcol = lambda j: cw[:, g * 4 + h, j:j + 1]
                    rk = ["xcb%d" % p, "xch%d" % p, "cw"]
                    ts("dve", ycv[:, g, :], xcp[:, g, 0:128], wcol(0), ALU.mult, rk, [K("ycv%d" % g)])
                    for j in range(1, 4):
                        stt(ycv[:, g, :], xcp[:, g, j:j + 128], wcol(j), ycv[:, g, :], ALU.mult, ALU.add, rk + [K("ycv%d" % g)], [K("ycv%d" % g)])
                if stop == "s_c2":
                    continue
                sg = lnt[:, 0:384].rearrange("p (g t) -> p g t", g=3)
                act(sg, ycv[:], AF.Exp, [K("ycv0"), K("ycv1"), K("ycv2")], [K("lnt")], scale=-1.0)
                act(sg, sg, AF.Ln, [K("lnt"), "cst"], [K("lnt")], bias=cst[:, 2:3])
                act(sg, sg, AF.Exp, [K("lnt")], [K("lnt")], scale=-1.0)
                tt("pool", ys[:], ycv[:], sg, ALU.mult, [K("ycv0"), K("ycv1"), K("ycv2"), K("lnt")], [K("ys")])
                tt("pool", y2[:], ys[:, 0:2, :], ys[:, 0:2, :], ALU.mult, [K("ys")], [K("y2")])
                mm(pS[:, 128:384], onesb[:], y2[:].rearrange("p g t -> p (g t)"), [K("y2"), "onesb"], ["pS"])
                rsqrt_act(rn[:].rearrange("p g t -> p (g t)"), pS[:, 128:384], ["pS"], [K("rn")], lnt[:, 0:256], K("lnt"), 1.0, cst[:, 0:1])
                if stop == "s_c3":
                    continue
                qnT, knT, vT = qnT_[p], knT_[p], vT_[p]
                stt(qnT[:], ys[:, 0, :], 128.0 ** -0.5, rn[:, 0, :], ALU.mult, ALU.mult, [K("ys"), K("rn")], [K("qnT")])
                tt("pool", knT[:], ys[:, 1, :], rn[:, 1, :], ALU.mult, [K("ys"), K("rn")], [K("knT")])
                cp("pool", vT[:], ys[:, 2, :], [K("ys")], [K("vT")])
                if stop == "s_c4":
                    continue
                tr(pT[:, 0:128], knT[:], idb[:], [K("knT"), "idb"], ["pT"])
                tr(pT[:, 128:256], vT[:], idb[:], [K("vT"), "idb"], ["pT"])
                ktok, vtok = ktok_[p], vtok_[p]
                cp("act", ktok[:], pT[:, 0:128], ["pT"], [K("ktok")])
                cp("act", vtok[:], pT[:, 128:256], ["pT"], [K("vtok")])
                if stop == "s_conv":
                    continue
                act(sc[:, 12:13], pB[:, 256:257], AF.Exp, ["pB", K("sc_r"), "cst"], [K("sc12")], scale=sc[:, 0:1])
                ts("dve", sc[:, 12:13], sc[:, 12:13], 1.0, ALU.add, [K("sc12")], [K("sc12")])
                S.op("dve", lambda e, o=sc[:, 13:14], i_=sc[:, 12:13]: e.reciprocal(out=o, in_=i_), [K("sc12")], [K("sc13")])
                ts("dve", sc[:, 1:2], sc[:, 13:14], -1.0, ALU.mult, [K("sc13")], [K("sc_beta")], s2=1.0, op1=ALU.add)
                if stop == "s_s1":
                    continue
                act(sc[:, 12:13], pB[:, 257:258], AF.Exp, ["pB", K("sc_r"), K("sc13"), "vec"], [K("sc12")], scale=sc[:, 0:1], bias=dtb_b[:, h:h + 1])
                act(sc[:, 13:14], sc[:, 12:13], AF.Ln, [K("sc12"), K("sc_beta"), "cst"], [K("sc13")], bias=cst[:, 2:3])
                ts("dve", sc[:, 2:3], sc[:, 13:14], nega[h], ALU.mult, [K("sc13"), "sm_a"], [K("sc_g")])
                if stop == "s_s2":
                    continue
                gU, eR, GT, DT, DTs = gU_[p], eR_[p], GT_[p], DT_[p], DTs_[p]
                act(gU[:], Uf, AF.Identity, ["cm", K("sc_g")], [K("gU")], scale=sc[:, 2:3])
                mm(pS[:, 384:512], onesf[:], gU[:], ["onesf", K("gU")], ["pS"])
                tt("dve", GT[:], pS[:, 384:512], idf, ALU.mult, ["pS", "cm"], [K("GT")])
                red(sc[:, 3:4], GT[:], ALU.add, [K("GT")], [K("sc_gc")])
                if stop == "s_s3":
                    continue
                cp("dve", sc[:, 6:7], pS[:, 511:512], ["pS"], [K("sc_gl")])
                act(eR[:], pS[:, 384:512], AF.Exp, ["pS"], [K("eR")])
                stt(GT[:], pS[:, 384:512], sc[:, 3:4], negm, ALU.subtract, ALU.add, ["pS", K("sc_gc"), "cm"], [K("GT")])
                act(DT[:], GT[:], AF.Exp, [K("GT")], [K("DT")])
                tt("pool", DTs[:], DT[:], strictm, ALU.mult, [K("DT"), "cm"], [K("DTs")])
                if stop == "s_s4":
                    continue
                act(sc[:, 12:13], sc[:, 3:4], AF.Exp, [K("sc_gc"), K("sc_g")], [K("sc12")])
                tt("dve", sc[:, 4:5], sc[:, 12:13], sc[:, 1:2], ALU.mult, [K("sc12"), K("sc_beta")], [K("sc_kbg")])
                act(sc[:, 5:6], sc[:, 3:4], AF.Exp, [K("sc_gc"), K("sc_gl")], [K("sc_kdec")], scale=-1.0, bias=sc[:, 6:7])
                act(sc[:, 7:8], sc[:, 6:7], AF.Exp, [K("sc_gl")], [K("sc_dec")])
                if stop == "s_scal":
                    continue
                kb, kbg, kdec, vb, kbT = kb_[p], kbg_[p], kdec_[p], vb_[p], kbT_[p]
                act(kbg[:], ktok[:], AF.Identity, [K("ktok"), K("sc_kbg")], [K("kbg")], scale=sc[:, 4:5])
                ts("dve", kdec[:], ktok[:], sc[:, 5:6], ALU.mult, [K("ktok"), K("sc_kdec")], [K("kdec")])
                act(vb[:], vtok[:], AF.Identity, [K("vtok"), K("sc_beta")], [K("vb")], scale=sc[:, 1:2])
                act(kb[:], idb[:], AF.Identity, ["idb", K("sc_beta")], [K("kb")], scale=sc[:, 1:2])
                mm(pS[:, 0:128], onesb[:], kb[:], ["onesb", K("kb")], ["pS"])
                tt("dve", kbT[:], knT[:], pS[:, 0:128], ALU.mult, [K("knT"), "pS"], [K("kbT")])
                qgT, AqkT = qgT_[p], AqkT_[p]
                tt("pool", qgT[:], qnT[:], eR[:], ALU.mult, [K("qnT"), K("eR")], [K("qgT")])
                if stop == "s_kvar":
                    continue
                mm(pG[:, 0:128], knT[:], kbT[:], [K("knT"), K("kbT")], ["pG"])
                mm(pG[:, 128:256], knT[:], qnT[:], [K("knT"), K("qnT")], ["pG"])
                q4 = b % 4
                nA, nB, PT = nA_[q4], nB_[q4], PT_[q4]
                K4 = lambda n: "%s%d" % (n, q4)
                stt(nB[:], pG[:, 0:128], -1.0, DTs[:], ALU.mult, ALU.mult, ["pG", K("DTs")], [K4("nB")])
                tt("dve", AqkT[:], pG[:, 128:256], DT[:], ALU.mult, ["pG", K("DT")], [K("AqkT")])
                mm(pG[:, 256:384], kbT[:], knT[:], [K("knT"), K("kbT")], ["pG"])
                G2, Ds = ksq_[p], kq_[p]
                stt(G2[:], pS[:, 384:512], sc[:, 3:4], posm, ALU.subtract, ALU.add, ["pS", K("sc_gc"), "cm"], [K("ksq")])
                act(Ds[:], G2[:], AF.Exp, [K("ksq")], [K("kq")], scale=-1.0)
                stt(nA[:], pG[:, 256:384], -1.0, Ds[:], ALU.mult, ALU.mult, ["pG", K("kq")], [K4("nA")])
                tt("pool", PT[:], nB[:], idb[:], ALU.add, [K4("nB"), "idb"], [K4("PT")])
                pNb, pNk = (pN, "pN") if b % 2 == 0 else (pM, "pM")
                for lvl in range(7):
                    if lvl >= 1:
                        mm(pNb[:, 256:384], nA[:], PT[:], [K4("nA"), K4("PT")], [pNk])
                    if lvl <= 4:
                        mm(pNb[:, 0:128], nA[:], nB[:], [K4("nA"), K4("nB")], [pNk])
                    if lvl <= 5:
                        mm(pNb[:, 128:256], nB[:], nA[:], [K4("nA"), K4("nB")], [pNk])
                    if lvl >= 1:
                        tt("dve", PT[:], pNb[:, 256:384], PT[:], ALU.add, [pNk, K4("PT")], [K4("PT")])
                    if lvl <= 4:
                        cp("act", nB[:], pNb[:, 0:128], [pNk], [K4("nB")])
                    if lvl <= 5:
                        cp("act", nA[:], pNb[:, 128:256], [pNk], [K4("nA")])
                if stop == "s_neu":
                    continue
                u, wT, vnew = u_[p], wT_[p], vnew_[p]
                mm(pNb[:, 384:512], PT[:], vb[:], [K4("PT"), K("vb")], [pNk])
                mm(pNb[:, 0:128], kbg[:], PT[:], [K4("PT"), K("kbg")], [pNk])
                cp("act", u[:], pNb[:, 384:512], [pNk], [K("u")])
                cp("act", wT[:], pNb[:, 0:128], [pNk], [K("wT")])
                if stop == "s_uw":
                    continue
                mm(pQ[:, 0:128], wT[:], Sb_[:], [K("wT"), "Sb"], ["pQ"])
                tt("dve", vnew[:], u[:], pQ[:, 0:128], ALU.subtract, [K("u"), "pQ"], [K("vnew")])
                mm(pQ[:, 128:256], qgT[:], Sb_[:], [K("qgT"), "Sb"], ["pQ"], start=True, stop=False)
                mm(pQ[:, 128:256], AqkT[:], vnew[:], [K("AqkT"), K("vnew")], ["pQ"], start=False, stop=True)
                mm(pQ[:, 256:384], kdec[:], vnew[:], [K("kdec"), K("vnew")], ["pQ"])
                stt(Sf[:], Sf[:], sc[:, 7:8], pQ[:, 256:384], ALU.mult, ALU.add, ["Sf", K("sc_dec"), "pQ"], ["Sf"])
                cp("act", Sb_[:], Sf[:], ["Sf"], ["Sb"])
                j, m = b // 8, b % 8
                if m == 0:
                    ts("dve", oacc[:, j, :], pQ[:, 128:256], sel[:, 0:1], ALU.mult, ["pQ", "sel"], ["oacc%d" % j])
                else:
                    stt(oacc[:, j, :], pQ[:, 128:256], sel[:, m:m + 1], oacc[:, j, :], ALU.mult, ALU.add, ["pQ", "sel", "oacc%d" % j], ["oacc%d" % j])
                if stop == "s_seq":
                    continue
            if h == 0:
                dbg_dump("KT", KT[:], [128, S_], ["KT%d" % b for b in range(NBLK)], BF16)
                dbg_dump("oacc", oacc[:], [128, NOWN, 128], ["oacc%d" % j for j in range(NOWN)])

            if stop is not None and stop.startswith("s"):
                break
            for i in range(NOWN):
                p = i % 2
                K = lambda n: "%s%d" % (n, p)
                sc, lnt = sc_[p], lnt_[p]
                tt("pool", obn[p][:], oacc[:, i, :], oacc[:, i, :], ALU.mult, ["oacc%d" % i], [K("obn")])
                red(sc[:, 8:9], obn[p][:], ALU.add, [K("obn")], [K("sc_ms")])
                rsqrt_act(sc[:, 10:11], sc[:, 8:9], [K("sc_ms")], [K("sc_rn")], lnt[:, 0:1], K("lnt"), 1.0 / 128, cst[:, 0:1])
                stt(obn[p][:], oacc[:, i, :], sc[:, 10:11], gnw_b, ALU.mult, ALU.mult, ["oacc%d" % i, K("sc_rn"), "vec", K("sc_ms")], [K("obn")])
                tt("dve", obb[p][:], obn[p][:], zas[:, i, :], ALU.mult, [K("obn"), "zas%d" % i], [K("obb")])
                tr(pT[:, 640:768], obb[p][:], idb[:], [K("obb"), "idb"], ["pT"])
                cp("act", oTs[p][:], pT[:, 640:768], ["pT"], [K("oTs")])
                S.dma(oscr[:, h, i * 128:(i + 1) * 128], oTs[p][:], reads=[K("oTs")], writes=["oscr_%d_%d" % (h, i)], slot="oscw%d" % p)

            if stop == "gdnfin":
                break
            gi = 0
            for i in range(NOWN):
                p = i % 2
                K = lambda n: "%s%d" % (n, p)
                sc, lnt = sc_[p], lnt_[p]
                nkb = 8 * (i + 1)
                for g0 in range(0, nkb, 2):
                    sp_, spk = (pA, "pA") if gi % 2 == 0 else (pB, "pB")
                    pt, ptk = Pt[gi % 3], "Pt%d" % (gi % 3)
                    gi += 1
                    for m in range(2):
                        kb_i = g0 + m
                        mm(sp_[:, m * 256:(m + 1) * 256], KT[:, kb_i * 128:(kb_i + 1) * 128], QT2[:, i, :],
                           ["KT%d" % kb_i, "QT%d" % i, "QT2z"], [spk])
                    act(pt[:], sp_[:], AF.Exp, [spk, "sm_bnd"], [ptk], scale=0.125, bias=nbnd_ap)
                    if g0 >= nkb - 8:
                        mo = g0 - (nkb - 8)
                        ptv = pt[:].rearrange("p (m c q) -> p m c q", m=2, c=2)
                        tt("dve", ptv, ptv, amb[:, mo:mo + 2, :].unsqueeze(2).broadcast_to([128, 2, 2, 128]), ALU.mult, [ptk, "amb"], [ptk])
                    for m in range(2):
                        kb_i = g0 + m
                        for c in range(2):
                            accp, acck = (pG, "pG") if c == 0 else (pN, "pN")
                            mm(accp[:, 0:129], pt[:, m * 256 + c * 128:m * 256 + (c + 1) * 128], Vg[:, kb_i, 0:129],
                               [ptk, "Vg%d" % kb_i, "Vg_ones"], [acck], start=(kb_i == 0), stop=(kb_i == nkb - 1))
                S.op("dve", lambda e, o=sc[:, 12:13], i_=pG[:, 128:129]: e.reciprocal(out=o, in_=i_), ["pG"], [K("sc12")])
                S.op("dve", lambda e, o=sc[:, 13:14], i_=pN[:, 128:129]: e.reciprocal(out=o, in_=i_), ["pN"], [K("sc13")])
                tt("dve", sc[:, 13:14], sc[:, 13:14], nlam_ap, ALU.mult, [K("sc13"), "sm_nlam"], [K("sc13")])
                ts("dve", t0[p][:], pG[:, 0:128], sc[:, 12:13], ALU.mult, ["pG", K("sc12")], [K("t0")])
                stt(ob[p][:], pN[:, 0:128], sc[:, 13:14], t0[p][:], ALU.mult, ALU.add, ["pN", K("sc13"), K("t0")], [K("ob")])
                tt("pool", obn[p][:], ob[p][:], ob[p][:], ALU.mult, [K("ob")], [K("obn")])
                red(sc[:, 8:9], obn[p][:], ALU.add, [K("obn")], [K("sc_ms")])
                rsqrt_act(sc[:, 10:11], sc[:, 8:9], [K("sc_ms")], [K("sc_rn")], lnt[:, 0:1], K("lnt"), 1.0 / 128, cst[:, 1:2])
                ts("dve", sc[:, 10:11], sc[:, 10:11], 1.0 - LAM_INIT, ALU.mult, [K("sc_rn")], [K("sc_rn")])
                stt(obn[p][:], ob[p][:], sc[:, 10:11], snw_b, ALU.mult, ALU.mult, [K("ob"), K("sc_rn"), "vec", K("sc_ms")], [K("obn")])
                tt("dve", obb[p][:], obn[p][:], zbs[:, i, :], ALU.mult, [K("obn"), "zbs%d" % i], [K("obb")])
                tr(pT[:, 640:768], obb[p][:], idb[:], [K("obb"), "idb"], ["pT"])
                cp("act", oTs[p][:], pT[:, 640:768], ["pT"], [K("oTs")])
                S.dma(oscr[:, 4 + h, i * 128:(i + 1) * 128], oTs[p][:], reads=[K("oTs")], writes=["oscr_%d_%d" % (4 + h, i)], slot="oscw%d" % p)

        wo_v = w_out.rearrange("(c p) n -> p c n", p=128)
        for gi_ in range(8):
            st, sk = Wst[0], "Wst0"
            S.dma(st[:, :, :], wo_v[:, :, gi_ * 128:(gi_ + 1) * 128], writes=[sk], slot=sk)
            cp("dve" if gi_ % 2 == 0 else "pool", Wb[:, :, gi_ * 128:(gi_ + 1) * 128], st[:, :, :], [sk], ["Wb"])
        oTl = [xb_[0], xb_[1]]
        xol = [sb("xol%d" % i, [128, D], F32)[:] for i in range(2)]
        outl = [sb("outl%d" % i, [128, D], F32)[:] for i in range(2)]
        fin = []
        if stop is not None:
            mset("pool", outl[0], 0.0, ["outl00", "outl10"])
            for i in range(NOWN):
                S.dma(outd[i * 128:(i + 1) * 128, :], outl[0], reads=["outl00", "outl10"], writes=["out%d" % i], slot="outw0")
                fin.append("out%d" % i)
        for i in range(NOWN if stop is None else 0):
            p = i % 2
            K = lambda n: "%s%d" % (n, p)
            S.dma(oTl[p][:], oscr[:, :, i * 128:(i + 1) * 128], reads=["oscr_%d_%d" % (ch, i) for ch in range(8)], writes=[K("xb")], slot="oTl%d" % p)
            S.dma(xol[p], xo[i * 128:(i + 1) * 128, :], writes=[K("xol")], slot="xol%d" % p)
            for half, (pp, ppk) in enumerate(((pA, "pA"), (pB, "pB"))):
                for ch in range(NCH):
                    mm(pp[:, :], oTl[p][:, ch, :], Wb[:, ch, half * 512:(half + 1) * 512], [K("xb"), "Wb"], [ppk], start=(ch == 0), stop=(ch == NCH - 1))
                tt("dve", outl[p][:, half * 512:(half + 1) * 512], pp[:, :], xol[p][:, half * 512:(half + 1) * 512], ALU.add,
                   [ppk, K("xol")], [K("outl%d" % half)])
            S.dma(outd[i * 128:(i + 1) * 128, :], outl[p], reads=[K("outl0"), K("outl1")], writes=["out%d" % i], slot="outw%d" % p)
            fin.append("out%d" % i)
        fin += ["dbgo_" + n for n in dbg_out]
        S.emit(final_keys=fin)
        nc._sched_stats = S.stats
    return nc


def host_inputs(inputs, NBLK):
    S_ = NBLK * 128
    x = np.asarray(inputs["x"], np.float32)[0, :S_]
    xTf = np.ascontiguousarray(x.T)
    g = lambda k: np.asarray(inputs[k], np.float32)[0]
    wnorm = np.ascontiguousarray(g("w_norm").reshape(NCH, 128).T)
    cwf = g("conv_w")
    convw = np.ascontiguousarray(cwf.reshape(4, 3, NH, 128).transpose(3, 1, 2, 0).reshape(128, 12, 4))
    vecs = np.concatenate([g("a_log"), g("dt_bias"), g("gdn_norm_w"), g("subln_w"), g("q_norm_w"), g("k_norm_w"),
                           g("lambda_q1"), g("lambda_k1"), g("lambda_q2"), g("lambda_k2")]).astype(np.float32)[None, :]
    inv_freq = (10000.0 ** (-np.arange(0, 64, 2, dtype=np.float32) / np.float32(64))).astype(np.float32)
    ang = np.arange(S_, dtype=np.float32)[:, None] * inv_freq[None, :]
    cos, sin = np.cos(ang).astype(np.float32), np.sin(ang).astype(np.float32)
    r = np.arange(128)
    U = (r[:, None] <= r[None, :]).astype(np.float32)
    cmask = np.stack([U, np.where(r[:, None] <= r[None, :], 0.0, NEG).astype(np.float32),
                      (r[:, None] < r[None, :]).astype(np.float32), np.eye(128, dtype=np.float32),
                      np.where(r[:, None] > r[None, :], 0.0, -NEG).astype(np.float32)], axis=1)
    w_in = np.ascontiguousarray(g("w_in"))
    w_out = np.ascontiguousarray(g("w_out"))
    maps = []
    for c in range(NCORE):
        blocks = np.arange(NBLK // NCORE) * 8 + c
        tok = (blocks[:, None] * 128 + r[None, :]).reshape(-1)
        am = np.zeros((128, 8, 128), np.float32)
        am[:, :c, :] = 1.0
        am[:, c, :] = U
        sel = np.zeros((128, 8), np.float32)
        sel[:, c] = 1.0
        maps.append(dict(xT=xTf, xTo=np.ascontiguousarray(xTf[:, tok]), xo=np.ascontiguousarray(x[tok]), w_in=w_in, w_out=w_out,
                         wnorm=wnorm, convw=convw, vecs=vecs, cosd=cos, sind=sin, coso=np.ascontiguousarray(cos[tok]),
                         sino=np.ascontiguousarray(sin[tok]), cmask=np.ascontiguousarray(cmask), amask=am, sel=sel))
    return maps


_NC_CACHE = {}


def run(inputs, NBLK, dbg=None, stop=None):
    key = (NBLK, tuple(sorted(dbg)) if dbg else None, stop)
    if key not in _NC_CACHE:
        _NC_CACHE[key] = build(NBLK, dbg, stop)
    nc = _NC_CACHE[key]
    maps = host_inputs(inputs, NBLK)
    res = run_bass_kernel_spmd(nc, maps, core_ids=list(range(NCORE)))
    S_ = NBLK * 128
    out = np.zeros((1, S_, D), np.float32)
    r = np.arange(128)
    for c in range(NCORE):
        blocks = np.arange(NBLK // NCORE) * 8 + c
        tok = (blocks[:, None] * 128 + r[None, :]).reshape(-1)
        out[0, tok] = res.results[c]["out"]
    return out, res


def kernel(**inputs):
    out, _ = run(inputs, 128)
    return out
```

```python
import math
from contextlib import ExitStack

import numpy as np
import ml_dtypes
import concourse.bass as bass
import concourse.mybir as mybir
from concourse.bass_utils import run_bass_kernel_spmd

F32 = mybir.dt.float32
BF16 = mybir.dt.bfloat16
AF = mybir.ActivationFunctionType
ALU = mybir.AluOpType
AX = mybir.AxisListType

D = 1024
NCH = 8
NH = 4
NCORE = 8
DIN = 4104
NEG = -30000.0
LAM_INIT = 0.8 - 0.6 * math.exp(-0.3 * 0)


class Sched:
    ENG = ("pe", "act", "dve", "pool", "sp")
    PSUM_KEYS = frozenset(("pA", "pB", "pS", "pT", "pG", "pN", "pM", "pQ"))
    XLAT = 0.5
    CP_PRIO = True

    def __init__(self, nc, es, reorder=True):
        self.nc, self.es, self.reorder = nc, es, reorder
        self.all = []
        self.last_w = {}
        self.readers = {}

    def sb(self, name, shape, dtype):
        return self.es.enter_context(self.nc.sbuf_tensor("s_" + name, list(shape), dtype))

    def ps(self, name, shape, dtype=F32):
        return self.es.enter_context(self.nc.psum_tensor("p_" + name, list(shape), dtype))

    def _add(self, eng, fn, reads, writes, dma_slot=None, cost=0.3, lat=None):
        op = dict(eng=eng, id=len(self.all), fn=fn, deps=set(), slot=dma_slot, cost=cost,
                  lat=(cost if lat is None else lat), succ=[])
        deps = op["deps"]
        for k in reads:
            w = self.last_w.get(k)
            if w is not None:
                deps.add(w["id"])
            if k in self.PSUM_KEYS:
                for r in self.readers.get(k, ()):
                    if r["eng"] != eng:
                        deps.add(r["id"])
        for k in writes:
            w = self.last_w.get(k)
            if w is not None:
                deps.add(w["id"])
            for r in self.readers.get(k, ()):
                deps.add(r["id"])
        for k in reads:
            self.readers.setdefault(k, []).append(op)
        for k in writes:
            self.last_w[k] = op
            self.readers[k] = []
        self.all.append(op)
        return op

    def op(self, eng, fn, reads=(), writes=(), cost=0.3):
        return self._add(eng, fn, list(reads), list(writes), cost=cost)

    def dma(self, out, in_, reads=(), writes=(), slot=None, eng="sp", nbytes=65536):
        return self._add(eng, lambda e: e.dma_start(out=out, in_=in_), list(reads), list(writes), dma_slot=slot,
                         cost=0.15, lat=2.5 + nbytes / 150e3)

    def _schedule(self):
        import heapq
        ops = self.all
        if not self.reorder:
            order = {e: [o for o in ops if o["eng"] == e] for e in self.ENG}
            return order
        indeg = [0] * len(ops)
        for o in ops:
            indeg[o["id"]] = len(o["deps"])
            for d in o["deps"]:
                ops[d]["succ"].append(o["id"])
        ready_t = [0.0] * len(ops)
        fin = [0.0] * len(ops)
        rank = [0.0] * len(ops)
        if self.CP_PRIO:
            for o in reversed(ops):
                best_s = 0.0
                for sidx in o["succ"]:
                    v = rank[sidx] + (self.XLAT if ops[sidx]["eng"] != o["eng"] else 0.02)
                    if v > best_s:
                        best_s = v
                rank[o["id"]] = o["lat"] + best_s
        cand = {e: [] for e in self.ENG}
        avail = {e: [] for e in self.ENG}
        free = {e: 0.0 for e in self.ENG}
        order = {e: [] for e in self.ENG}
        for o in ops:
            if indeg[o["id"]] == 0:
                heapq.heappush(cand[o["eng"]], (0.0, o["id"]))
        left = len(ops)
        while left:
            best, be = None, None
            for e in self.ENG:
                if avail[e]:
                    t = free[e]
                elif cand[e]:
                    t = max(free[e], cand[e][0][0])
                else:
                    continue
                if best is None or t < best:
                    best, be = t, e
            assert be is not None, "scheduler stuck (cyclic deps?)"
            e = be
            while cand[e] and cand[e][0][0] <= best:
                ci = heapq.heappop(cand[e])[1]
                heapq.heappush(avail[e], (-rank[ci], ci))
            i = heapq.heappop(avail[e])[1]
            o = ops[i]
            free[e] = best + o["cost"]
            fin[i] = best + o["lat"]
            order[e].append(o)
            left -= 1
            for sidx in o["succ"]:
                so = ops[sidx]
                rt = fin[i] + (self.XLAT if so["eng"] != e else 0.02)
                if rt > ready_t[sidx]:
                    ready_t[sidx] = rt
                indeg[sidx] -= 1
                if indeg[sidx] == 0:
                    heapq.heappush(cand[so["eng"]], (ready_t[sidx], sidx))
        self.model_us = max(free.values())
        return order

    def emit(self, final_keys=()):
        nc = self.nc
        self._add("sp", None, list(final_keys), [], cost=0.0)
        order = self._schedule()
        ops = self.all
        pos = {}
        for e in self.ENG:
            for n_, o in enumerate(order[e]):
                pos[o["id"]] = n_
                o["signal"] = False
        for o in ops:
            need = {}
            for d in o["deps"]:
                do = ops[d]
                if do["slot"] is not None:
                    need[("dma", d)] = d
                elif do["eng"] == "pe" and o["eng"] == "pe":
                    continue
                else:
                    cur = need.get(do["eng"])
                    if cur is None or pos[cur] < pos[d]:
                        need[do["eng"]] = d
            o["need"] = sorted(need.values(), key=lambda d: (ops[d]["eng"], pos[d]))
            for d in o["need"]:
                ops[d]["signal"] = True
        ROT = 3000
        slot_sem, slot_cnt = {}, {}
        nsem = ndsem = 0
        for e in self.ENG:
            cnt, cur = 0, None
            for op in order[e]:
                if op["slot"] is not None:
                    s = op["slot"]
                    if s not in slot_sem or slot_cnt[s] + 16 > ROT:
                        slot_sem[s] = self.es.enter_context(nc.semaphore("dsem_%d" % ndsem))
                        ndsem += 1
                        slot_cnt[s] = 0
                    slot_cnt[s] += 16
                    op["sig"] = (slot_sem[s], slot_cnt[s])
                elif op["signal"]:
                    if cur is None or cnt >= ROT:
                        cur = self.es.enter_context(nc.semaphore("sem_%s_%d" % (e, nsem)))
                        nsem += 1
                        cnt = 0
                    cnt += 1
                    op["sig"] = (cur, cnt)
        self.stats = {e: (len(order[e]), sum(1 for o in order[e] if o["signal"])) for e in self.ENG}
        self.stats["dma_sems"] = ndsem
        self.stats["engine_sems"] = nsem
        self.stats["model_us"] = getattr(self, "model_us", None)
        self.stats["busy_us"] = {e: round(sum(o["cost"] for o in order[e]), 1) for e in self.ENG}
        block = self.es.enter_context(nc.Block())
        handles = dict(pe="tensor", act="scalar", dve="vector", pool="gpsimd", sp="sync")

        def make(e):
            def body(engh):
                seen = {}
                for op in order[e]:
                    for d in op["need"]:
                        s, v = ops[d]["sig"]
                        if seen.get(id(s), 0) < v:
                            engh.wait_ge(s, v)
                            seen[id(s)] = v
                    if op["fn"] is None:
                        continue
                    ins = op["fn"](engh)
                    if op["slot"] is not None:
                        ins.then_inc(op["sig"][0], 16)
                    elif op["signal"]:
                        ins.then_inc(op["sig"][0], 1)
            return body

        for e in self.ENG:
            getattr(block, handles[e])(make(e))


def build(NBLK, dbg=None, stop=None, reorder=True):
    S_ = NBLK * 128
    NOWN = NBLK // NCORE
    SO = NOWN * 128
    NT4 = NBLK // 4
    nc = bass.Bass("TRN2", target_bir_lowering=False)
    es = ExitStack()
    dt = nc.dram_tensor
    xT = dt("xT", [D, S_], F32, kind="ExternalInput").ap()
    xTo = dt("xTo", [D, SO], F32, kind="ExternalInput").ap()
    xo = dt("xo", [SO, D], F32, kind="ExternalInput").ap()
    w_in = dt("w_in", [D, DIN], F32, kind="ExternalInput").ap()
    w_out = dt("w_out", [D, D], F32, kind="ExternalInput").ap()
    wnorm = dt("wnorm", [128, NCH], F32, kind="ExternalInput").ap()
    convw = dt("convw", [128, 12, 4], F32, kind="ExternalInput").ap()
    vecs = dt("vecs", [1, 8 + 128 * 2 + 64 * 6], F32, kind="ExternalInput").ap()
    cosd = dt("cosd", [S_, 32], F32, kind="ExternalInput").ap()
    sind = dt("sind", [S_, 32], F32, kind="ExternalInput").ap()
    coso = dt("coso", [SO, 32], F32, kind="ExternalInput").ap()
    sino = dt("sino", [SO, 32], F32, kind="ExternalInput").ap()
    cmask = dt("cmask", [128, 5, 128], F32, kind="ExternalInput").ap()
    amaskd = dt("amask", [128, 8, 128], F32, kind="ExternalInput").ap()
    seld = dt("sel", [128, 8], F32, kind="ExternalInput").ap()
    outd = dt("out", [SO, D], F32, kind="ExternalOutput").ap()
    oscr = dt("oscr", [128, NCH, SO], BF16, kind="Internal").ap()
    dbg_out = {}

    with es:
        S = Sched(nc, es, reorder)
        sb, ps = S.sb, S.ps

        def fsz(ap):
            n = 1
            for d in list(ap.shape)[1:]:
                n *= int(d)
            return n

        def mm(out, lhsT, rhs, r, w, start=True, stop=True):
            c = 0.064 + fsz(rhs) / 2400.0
            if rhs.dtype == F32:
                c *= 4
            S.op("pe", lambda e: e.matmul(out, lhsT=lhsT, rhs=rhs, start=start, stop=stop), r, w, cost=c)

        def tr(out, in_, ident, r, w):
            S.op("pe", lambda e: e.transpose(out=out, in_=in_, identity=ident), r, w, cost=0.12)

        def act(out, in_, func, r, w, scale=None, bias=None):
            kw = {}
            if scale is not None:
                kw["scale"] = scale
            if bias is not None:
                kw["bias"] = bias
            S.op("act", lambda e: e.activation(out=out, in_=in_, func=func, **kw), r, w, cost=0.22 + fsz(out) / 1400.0)

        def ecost(eng, out):
            n = fsz(out)
            return {"dve": 0.12 + n / 960.0, "pool": 0.35 + n / 500.0, "act": 0.22 + n / 1400.0}[eng]

        def tt(eng, out, in0, in1, op, r, w):
            S.op(eng, lambda e: e.tensor_tensor(out=out, in0=in0, in1=in1, op=op), r, w, cost=ecost(eng, out))

        def ts(eng, out, in0, s1, op0, r, w, s2=None, op1=None):
            if op1 is None:
                S.op(eng, lambda e: e.tensor_scalar(out=out, in0=in0, scalar1=s1, scalar2=None, op0=op0), r, w, cost=ecost(eng, out))
            else:
                S.op(eng, lambda e: e.tensor_scalar(out=out, in0=in0, scalar1=s1, scalar2=s2, op0=op0, op1=op1), r, w, cost=ecost(eng, out))

        def stt(out, in0, scalar, in1, op0, op1, r, w):
            S.op("dve", lambda e: e.scalar_tensor_tensor(out=out, in0=in0, scalar=scalar, in1=in1, op0=op0, op1=op1), r, w,
                 cost=0.12 + fsz(out) / 480.0)

        def cp(eng, out, in_, r, w):
            if eng == "act":
                S.op("act", lambda e: e.copy(out=out, in_=in_), r, w, cost=ecost("act", out))
            else:
                S.op(eng, lambda e: e.tensor_copy(out=out, in_=in_), r, w, cost=ecost(eng, out))

        def red(out, in_, op, r, w):
            S.op("dve", lambda e: e.tensor_reduce(out=out, in_=in_, axis=AX.X, op=op), r, w, cost=0.12 + fsz(in_) / 960.0)

        def mset(eng, ap, val, w):
            S.op(eng, lambda e: e.memset(ap, val), [], w, cost=ecost(eng, ap))

        def rsqrt_act(out, in_, r, w, tmp, tmpk, scale, bias_ap):
            act(tmp, in_, AF.Ln, list(r) + ["cst"], [tmpk], scale=scale, bias=bias_ap)
            act(out, tmp, AF.Exp, [tmpk], w, scale=-0.5)

        def dbg_dump(name, ap_sb, shape, rkeys, dtype=F32):
            if dbg is None or name not in dbg:
                return
            d = dt("dbg_" + name, list(shape), dtype, kind="ExternalOutput").ap()
            dbg_out[name] = d
            S.dma(d, ap_sb, reads=rkeys, writes=["dbgo_" + name], slot="dbg_" + name)

        cm = sb("cm", [128, 5, 128], F32)
        S.dma(cm[:], cmask, writes=["cm"], slot="c_cm")
        Uf, negm, strictm, idf, posm = cm[:, 0, :], cm[:, 1, :], cm[:, 2, :], cm[:, 3, :], cm[:, 4, :]
        idb = sb("idb", [128, 128], BF16)
        cp("dve", idb[:], idf, ["cm"], ["idb"])
        onesf = sb("onesf", [128, 128], F32)
        onesb = sb("onesb", [128, 128], BF16)
        mset("pool", onesf[:], 1.0, ["onesf"])
        mset("pool", onesb[:], 1.0, ["onesb"])
        cst = sb("cst", [128, 4], F32)
        for i_, v_ in enumerate((1e-6, 1e-5, 1.0, 0.0)):
            mset("dve", cst[:, i_:i_ + 1], v_, ["cst"])
        NV = 8 + 256 + 384
        vec = sb("vec", [128, NV], F32)
        S.dma(vec[:], vecs[0:1, :].broadcast_to([128, NV]), writes=["vec"], slot="c_vec")
        a_log_b, dtb_b = vec[:, 0:4], vec[:, 4:8]
        gnw_b, snw_b = vec[:, 8:136], vec[:, 136:264]
        qnw_b, knw_b = vec[:, 264:328], vec[:, 328:392]
        lq1, lk1, lq2, lk2 = vec[:, 392:456], vec[:, 456:520], vec[:, 520:584], vec[:, 584:648]
        wn = sb("wn", [128, NCH], F32)
        S.dma(wn[:], wnorm, writes=["wn"], slot="c_wn")
        cw = sb("cw", [128, 12, 4], F32)
        S.dma(cw[:], convw, writes=["cw"], slot="c_cw")
        sel = sb("sel", [128, 8], F32)
        S.dma(sel[:], seld, writes=["sel"], slot="c_sel")
        coso_t = sb("coso_t", [128, NOWN, 32], F32)
        sino_t = sb("sino_t", [128, NOWN, 32], F32)
        S.dma(coso_t[:], coso.rearrange("(i p) d -> p i d", p=128), writes=["coso"], slot="c_coso")
        S.dma(sino_t[:], sino.rearrange("(i p) d -> p i d", p=128), writes=["sino"], slot="c_sino")

        sm = sb("sm", [128, 16], F32)
        act(sm[:, 0:4], a_log_b, AF.Exp, ["vec"], ["sm_a"])
        ts("dve", sm[:, 0:4], sm[:, 0:4], -1.0, ALU.mult, ["sm_a"], ["sm_a"])
        ltmp = sb("ltmp", [128, 64], F32)
        tt("dve", ltmp[:], lq1, lk1, ALU.mult, ["vec"], ["ltmp"])
        red(sm[:, 8:9], ltmp[:], ALU.add, ["ltmp"], ["sm_l1"])
        tt("dve", ltmp[:], lq2, lk2, ALU.mult, ["vec", "sm_l1"], ["ltmp"])
        red(sm[:, 9:10], ltmp[:], ALU.add, ["ltmp"], ["sm_l2"])
        act(sm[:, 8:10], sm[:, 8:10], AF.Exp, ["sm_l1", "sm_l2"], ["sm_l12"])
        tt("dve", sm[:, 4:5], sm[:, 8:9], sm[:, 9:10], ALU.subtract, ["sm_l12"], ["sm_lam"])
        ts("dve", sm[:, 4:5], sm[:, 4:5], LAM_INIT, ALU.add, ["sm_lam"], ["sm_lam"])
        ts("dve", sm[:, 5:6], sm[:, 4:5], -1.0, ALU.mult, ["sm_lam"], ["sm_nlam"])
        S.op("dve", lambda e: e.tensor_reduce(out=sm[:, 10:11], in_=qnw_b, axis=AX.X, op=ALU.max, apply_absolute_value=True), ["vec"], ["sm_mq"])
        S.op("dve", lambda e: e.tensor_reduce(out=sm[:, 11:12], in_=knw_b, axis=AX.X, op=ALU.max, apply_absolute_value=True), ["vec"], ["sm_mk"])
        tt("dve", sm[:, 6:7], sm[:, 10:11], sm[:, 11:12], ALU.mult, ["sm_mq", "sm_mk"], ["sm_bnd"])
        ts("dve", sm[:, 6:7], sm[:, 6:7], -8.0, ALU.mult, ["sm_bnd"], ["sm_bnd"])
        nega = [sm[:, h:h + 1] for h in range(NH)]
        lam_ap, nlam_ap, nbnd_ap = sm[:, 4:5], sm[:, 5:6], sm[:, 6:7]
        qnw2 = sb("qnw2", [128, 2, 64], F32)
        knw2 = sb("knw2", [128, 2, 64], F32)
        for c_ in range(2):
            cp("pool", qnw2[:, c_, :], qnw_b, ["vec"], ["qnw2"])
            cp("pool", knw2[:, c_, :], knw_b, ["vec"], ["knw2"])

        pA = ps("pA", [128, 512]); pB = ps("pB", [128, 512]); pS = ps("pS", [128, 512])
        pT = ps("pT", [128, 1024], BF16); pG = ps("pG", [128, 512]); pN = ps("pN", [128, 512])
        pQ = ps("pQ", [128, 512]); pM = ps("pM", [128, 512])

        NCOL = 1026
        Wb = sb("Wb", [128, NCH, NCOL], BF16)
        Wst = [sb("Wst0", [128, NCH, 128], F32)] * 2
        KT = sb("KT", [128, S_], BF16)
        Vg = sb("Vg", [128, NBLK, 130], BF16)
        mset("pool", Vg[:, :, 128:130], 1.0, ["Vg_ones"])
        QT2 = sb("QT2", [128, NOWN, 256], BF16)
        mset("pool", QT2[:], 0.0, ["QT2z"])
        zas = sb("zas", [128, NOWN, 128], BF16)
        zbs = sb("zbs", [128, NOWN, 128], BF16)
        oacc = sb("oacc", [128, NOWN, 128], F32)
        xTt = [sb("xTt%d" % i, [128, NCH, 256], F32) for i in range(2)]
        cst4 = [sb("cs4_%d" % i, [128, 2, 2, 32], F32) for i in range(2)]
        amb = sb("amb", [128, 8, 128], BF16)
        amf = xTt[0][:, 0:4, :].rearrange("p c (m q) -> p (c m) q", q=128)
        S.dma(amf, amaskd, writes=["xTt0"], slot="c_am")
        cp("dve", amb[:], amf, ["xTt0"], ["amb"])
        Sf = sb("Sf", [128, 128], F32)
        Sb_ = sb("Sb", [128, 128], BF16)
        xc = [sb("xc%d" % i, [128, 3, 131], F32) for i in range(3)]

        NB = 2

        def dbuf(name, shape, dtype, n=None):
            n = NB if n is None else n
            return [sb("%s%d" % (name, i), shape, dtype) for i in range(n)]

        xb_ = dbuf("xb", [128, NCH, 128], BF16)
        rbc_ = dbuf("rbc", [128, 128], F32)
        lnt_ = dbuf("lnt", [128, 384], F32)
        sc_ = dbuf("sc", [128, 16], F32)
        ycv_ = dbuf("ycv", [128, 3, 128], F32)
        ys_ = dbuf("ys", [128, 3, 128], F32)
        y2_ = dbuf("y2", [128, 2, 128], BF16)
        rn_ = dbuf("rn", [128, 2, 128], F32)
        qnT_ = dbuf("qnT", [128, 128], BF16)
        knT_ = dbuf("knT", [128, 128], BF16)
        vT_ = dbuf("vT", [128, 128], BF16)
        kvtok_ = dbuf("kvtok", [128, 256], BF16)
        ktok_ = [t[:, 0:128] for t in kvtok_]
        vtok_ = [t[:, 128:256] for t in kvtok_]
        kb_ = dbuf("kb", [128, 128], BF16)
        kbg_ = dbuf("kbg", [128, 128], BF16)
        kdec_ = dbuf("kdec", [128, 128], BF16)
        vb_ = dbuf("vb", [128, 128], BF16)
        kbT_ = dbuf("kbT", [128, 128], BF16)
        gU_ = dbuf("gU", [128, 128], F32)
        eR_ = dbuf("eR", [128, 128], F32)
        GT_ = dbuf("GT", [128, 128], F32)
        DT_ = dbuf("DT", [128, 128], F32)
        DTs_ = dbuf("DTs", [128, 128], F32)
        qgT_ = dbuf("qgT", [128, 128], BF16)
        AqkT_ = dbuf("AqkT", [128, 128], BF16)
        nAB_ = dbuf("nAB", [128, 256], BF16, 4)
        nA_ = [t[:, 128:256] for t in nAB_]
        nB_ = [t[:, 0:128] for t in nAB_]
        PT_ = dbuf("PT", [128, 128], BF16, 4)
        u_ = dbuf("u", [128, 128], F32)
        wT_ = dbuf("wT", [128, 128], BF16)
        vnew_ = dbuf("vnew", [128, 128], BF16)
        kq_ = dbuf("kq", [128, 128], F32)
        ksq_ = dbuf("ksq", [128, 128], F32)
        kn_ = dbuf("kn", [128, 128], F32)
        rt_ = dbuf("rt", [128, 4, 64], F32)
        kr_ = dbuf("kr", [128, 128], BF16)
        obn = dbuf("obn", [128, 128], F32, 2)
        obb = dbuf("obb", [128, 128], BF16, 2)
        oTs = dbuf("oTs", [128, 128], BF16, 2)
        Pt = dbuf("Pt", [128, 512], BF16, 3)
        t0 = dbuf("t0", [128, 128], F32, 2)
        ob = dbuf("ob", [128, 128], F32, 2)

        def load_weights(h):
            groups = [h * 128, 512 + h * 128, 1024 + h * 128, 2568 + h * 128, 3080 + h * 128,
                      None, 2056 + h * 128, 1536 + h * 128, 3592 + h * 128]
            dst = [0, 128, 256, 384, 512, 640, 642, 770, 898]
            w_v = w_in.rearrange("(c p) n -> p c n", p=128)
            for gi, (c0, d0) in enumerate(zip(groups, dst)):
                st = Wst[gi % 2]
                sk = "Wst0"
                if c0 is None:
                    S.dma(st[:, :, 0:8], w_v[:, :, 2048:2056], writes=[sk], slot=sk)
                    for q_, src_c in enumerate((h, 4 + h)):
                        tt("dve", Wb[:, :, d0 + q_:d0 + q_ + 1], st[:, :, src_c:src_c + 1],
                           wn[:].unsqueeze(2), ALU.mult, [sk, "wn"], ["Wb"])
                    continue
                else:
                    S.dma(st[:, :, :], w_v[:, :, c0:c0 + 128], writes=[sk], slot=sk)
                    n = 128
                tt("dve" if gi % 2 == 0 else "pool", Wb[:, :, d0:d0 + n], st[:, :, 0:n],
                   wn[:].unsqueeze(2).broadcast_to([128, NCH, n]), ALU.mult, [sk, "wn"], ["Wb"])

        def rope_norm(src_ps, src_key, rstd_ap, rstd_key, w2, w2key, cos_ap, sin_ap, cskeys, p, outT_ap, outT_keys, pt_slot):
            kq, ksq, kn, rt, kr, sc, lnt = kq_[p], ksq_[p], kn_[p], rt_[p], kr_[p], sc_[p], lnt_[p]
            K = lambda n: "%s%d" % (n, p)
            act(kq[:], src_ps, AF.Identity, [src_key, rstd_key], [K("kq")], scale=rstd_ap)
            tt("pool", ksq[:], kq[:], kq[:], ALU.mult, [K("kq")], [K("ksq")])
            red(sc[:, 8:10], ksq[:].rearrange("p (c d) -> p c d", c=2), ALU.add, [K("ksq")], [K("sc_ms")])
            rsqrt_act(sc[:, 10:12], sc[:, 8:10], [K("sc_ms")], [K("sc_rn")], lnt[:, 0:2], K("lnt"), 1.0 / 64, cst[:, 0:1])
            tt("pool", kn[:].rearrange("p (c d) -> p c d", c=2), kq[:].rearrange("p (c d) -> p c d", c=2),
               sc[:, 10:12].unsqueeze(2).broadcast_to([128, 2, 64]), ALU.mult, [K("kq"), K("sc_rn")], [K("kn")])
            tt("pool", kn[:].rearrange("p (c d) -> p c d", c=2), kn[:].rearrange("p (c d) -> p c d", c=2), w2[:], ALU.mult,
               [K("kn"), w2key], [K("kn")])
            knv = kn[:].rearrange("p (c t d) -> p c t d", c=2, t=2)
            x1, x2 = knv[:, :, 0, :], knv[:, :, 1, :]
            cb = cos_ap.unsqueeze(1).broadcast_to([128, 2, 32])
            sbc = sin_ap.unsqueeze(1).broadcast_to([128, 2, 32])
            rtv = rt[:].rearrange("p f (c d) -> p f c d", c=2)
            tt("dve", rtv[:, 0], x1, cb, ALU.mult, [K("kn")] + cskeys, [K("rt0")])
            tt("pool", rtv[:, 1], x2, sbc, ALU.mult, [K("kn")] + cskeys, [K("rt1")])
            tt("dve", rtv[:, 2], x2, cb, ALU.mult, [K("kn")] + cskeys, [K("rt2")])
            tt("pool", rtv[:, 3], x1, sbc, ALU.mult, [K("kn")] + cskeys, [K("rt3")])
            krv = kr[:].rearrange("p (c t d) -> p c t d", c=2, t=2)
            tt("dve", krv[:, :, 0, :], rtv[:, 0], rtv[:, 1], ALU.subtract, [K("rt0"), K("rt1")], [K("kr_a")])
            tt("pool", krv[:, :, 1, :], rtv[:, 2], rtv[:, 3], ALU.add, [K("rt2"), K("rt3")], [K("kr_b")])
            tr(pT[:, pt_slot * 128:(pt_slot + 1) * 128], kr[:], idb[:], [K("kr_a"), K("kr_b"), "idb"], ["pT"])
            if isinstance(outT_ap, list):
                for (dst, r0, r1) in outT_ap:
                    cp("act", dst, pT[r0:r1, pt_slot * 128:(pt_slot + 1) * 128], ["pT", "QT2z"], outT_keys)
            else:
                cp("act", outT_ap, pT[:, pt_slot * 128:(pt_slot + 1) * 128], ["pT"], outT_keys)

        rstd_all = sb("rstd_all", [128, NBLK], F32)
        rstd_own = sb("rstd_own", [128, NOWN], F32)

        def x_block_prep(src_tile_ap, src_key, p, cache_ap, cache_key, first):
            xb, rbc, lnt, sc, dg = xb_[p], rbc_[p], lnt_[p], sc_[p], gU_[p]
            K = lambda n: "%s%d" % (n, p)
            cp("dve", xb[:], src_tile_ap, [src_key], [K("xb")])
            if first:
                for c in range(NCH):
                    mm(pB[:, 384:512], xb[:, c, :], xb[:, c, :], [K("xb")], ["pB"], start=(c == 0), stop=(c == NCH - 1))
                tt("dve", lnt[:, 256:384], pB[:, 384:512], idf, ALU.mult, ["pB", "cm"], [K("lnt")])
                red(sc[:, 14:15], lnt[:, 256:384], ALU.add, [K("lnt")], [K("sc14")])
                rsqrt_act(cache_ap, sc[:, 14:15], [K("sc14")], [cache_key], sc[:, 15:16], K("sc15"), 1.0 / D, cst[:, 0:1])
            cp("pool", sc[:, 0:1], cache_ap, [cache_key], [K("sc_r")])
            act(dg[:], idf, AF.Identity, ["cm", cache_key], [K("gU")], scale=cache_ap)
            mm(pB[:, 384:512], onesf[:], dg[:], ["onesf", K("gU")], ["pB"])
            cp("act", rbc[:], pB[:, 384:512], ["pB"], [K("rbc")])

        for h in range(NH):
            if stop == "const":
                break
            load_weights(h)
            if stop == "weights":
                break
            xTo_v = xTo.rearrange("(c p) t -> p c t", p=128)
            for i in range(NOWN):
                p = i % 2
                K = lambda n: "%s%d" % (n, p)
                tl, tk = xTt[p], "xTt%d" % p
                S.dma(tl[:, :, 0:128], xTo_v[:, :, i * 128:(i + 1) * 128], writes=[tk], slot=tk)
                x_block_prep(tl[:, :, 0:128], tk, p, rstd_own[:, i:i + 1], "rstd_own%d" % i, h == 0)
                for c in range(NCH):
                    mm(pB[:, 0:384], xb_[p][:, c, :], Wb[:, c, 642:1026], [K("xb"), "Wb"], ["pB"], start=(c == 0), stop=(c == NCH - 1))
                rope_norm(pB[:, 0:128], "pB", sc_[p][:, 0:1], K("sc_r"), qnw2, "qnw2", coso_t[:, i, :], sino_t[:, i, :],
                          ["coso", "sino"], p, [(QT2[0:64, i, 0:128], 0, 64), (QT2[64:128, i, 128:256], 64, 128)], ["QT%d" % i], 0)
                act(zas[:, i, :], pB[:, 128:256], AF.Silu, ["pB", K("sc_r")], ["zas%d" % i], scale=sc_[p][:, 0:1])
                act(zbs[:, i, :], pB[:, 256:384], AF.Silu, ["pB", K("sc_r")], ["zbs%d" % i], scale=sc_[p][:, 0:1])
            if stop == "own":
                break
            mset("pool", Sf[:], 0.0, ["Sf"])
            mset("pool", Sb_[:], 0.0, ["Sb"])
            mset("pool", xc[0][:, :, 0:3], 0.0, ["xch0"])
            xT_v = xT.rearrange("(c p) t -> p c t", p=128)
            for b in range(NBLK):
                p = b % NB
                K = lambda n: "%s%d" % (n, p)
                t4, tp = b // 2, (b // 2) % 2
                tl, tk = xTt[tp], "xTt%d" % tp
                cs4, csk = cst4[tp], "cs4_%d" % tp
                if b % 2 == 0:
                    S.dma(tl[:], xT_v[:, :, t4 * 256:(t4 + 1) * 256], writes=[tk], slot=tk)
                    S.dma(cs4[:, 0], cosd[t4 * 256:(t4 + 1) * 256, :].rearrange("(i p) d -> p i d", p=128), writes=[csk], slot=csk)
                    S.dma(cs4[:, 1], sind[t4 * 256:(t4 + 1) * 256, :].rearrange("(i p) d -> p i d", p=128), writes=[csk], slot=csk)
                bo = (b % 2) * 128
                x_block_prep(tl[:, :, bo:bo + 128], tk, p, rstd_all[:, b:b + 1], "rstd_all%d" % b, h == 0)
                xb, rbc, sc, lnt = xb_[p], rbc_[p], sc_[p], lnt_[p]
                for g in range(3):
                    for c in range(NCH):
                        mm(pA[:, g * 128:(g + 1) * 128], Wb[:, c, g * 128:(g + 1) * 128], xb[:, c, :], [K("xb"), "Wb"], ["pA"],
                           start=(c == 0), stop=(c == NCH - 1))
                for c in range(NCH):
                    mm(pB[:, 0:258], xb[:, c, :], Wb[:, c, 384:642], [K("xb"), "Wb"], ["pB"], start=(c == 0), stop=(c == NCH - 1))
                if stop == "s_proj":
                    continue
                rope_norm(pB[:, 0:128], "pB", sc[:, 0:1], K("sc_r"), knw2, "knw2", cs4[:, 0, b % 2, :], cs4[:, 1, b % 2, :],
                          [csk], p, KT[:, b * 128:(b + 1) * 128], ["KT%d" % b], 4)
                act(Vg[:, b, 0:128], pB[:, 128:256], AF.Identity, ["pB", K("sc_r")], ["Vg%d" % b], scale=sc[:, 0:1])
                pn = (b + 1) % NB
                xcp, xcn = xc[p], xc[pn]
                tt("dve", xcp[:, :, 3:131], pA[:, 0:384].rearrange("p (g t) -> p g t", g=3),
                   rbc[:].unsqueeze(1).broadcast_to([128, 3, 128]), ALU.mult, ["pA", K("rbc")], ["xcb%d" % p])
                cp("pool", xcn[:, :, 0:3], xcp[:, :, 128:131], ["xcb%d" % p], ["xch%d" % pn])
                if stop == "s_c1":
                    continue
                ycv, ys, y2, rn = ycv_[p], ys_[p], y2_[p], rn_[p]
                for g in range(3):
                    wcol = lambda j: cw[:, g * 4 + h, j:j + 1]
                    rk = ["xcb%d" % p, "xch%d" % p, "cw"]
                    ts("dve", ycv[:, g, :], xcp[:, g, 0:128], wcol(0), ALU.mult, rk, [K("ycv%d" % g)])
                    for j in range(1, 4):
                        stt(ycv[:, g, :], xcp[:, g, j:j + 128], wcol(j), ycv[:, g, :], ALU.mult, ALU.add, rk + [K("ycv%d" % g)], [K("ycv%d" % g)])
                if stop == "s_c2":
                    continue
                sg = lnt[:, 0:384].rearrange("p (g t) -> p g t", g=3)
                act(sg, ycv[:], AF.Exp, [K("ycv0"), K("ycv1"), K("ycv2")], [K("lnt")], scale=-1.0)
                act(sg, sg, AF.Ln, [K("lnt"), "cst"], [K("lnt")], bias=cst[:, 2:3])
                act(sg, sg, AF.Exp, [K("lnt")], [K("lnt")], scale=-1.0)
                tt("pool", ys[:], ycv[:], sg, ALU.mult, [K("ycv0"), K("ycv1"), K("ycv2"), K("lnt")], [K("ys")])
                tt("pool", y2[:], ys[:, 0:2, :], ys[:, 0:2, :], ALU.mult, [K("ys")], [K("y2")])
                mm(pS[:, 128:384], onesb[:], y2[:].rearrange("p g t -> p (g t)"), [K("y2"), "onesb"], ["pS"])
                rsqrt_act(rn[:].rearrange("p g t -> p (g t)"), pS[:, 128:384], ["pS"], [K("rn")], lnt[:, 0:256], K("lnt"), 1.0, cst[:, 0:1])
                if stop == "s_c3":
                    continue
                qnT, knT, vT = qnT_[p], knT_[p], vT_[p]
                stt(qnT[:], ys[:, 0, :], 128.0 ** -0.5, rn[:, 0, :], ALU.mult, ALU.mult, [K("ys"), K("rn")], [K("qnT")])
                tt("pool", knT[:], ys[:, 1, :], rn[:, 1, :], ALU.mult, [K("ys"), K("rn")], [K("knT")])
                cp("pool", vT[:], ys[:, 2, :], [K("ys")], [K("vT")])
                if stop == "s_c4":
                    continue
                tr(pT[:, 0:128], knT[:], idb[:], [K("knT"), "idb"], ["pT"])
                tr(pT[:, 128:256], vT[:], idb[:], [K("vT"), "idb"], ["pT"])
                ktok, vtok = ktok_[p], vtok_[p]
                cp("act", kvtok_[p][:, 0:256], pT[:, 0:256], ["pT"], [K("ktok"), K("vtok")])
                if stop == "s_conv":
                    continue
                act(sc[:, 12:13], pB[:, 256:257], AF.Exp, ["pB", K("sc_r"), "cst"], [K("sc12")], scale=sc[:, 0:1])
                ts("dve", sc[:, 12:13], sc[:, 12:13], 1.0, ALU.add, [K("sc12")], [K("sc12")])
                S.op("dve", lambda e, o=sc[:, 13:14], i_=sc[:, 12:13]: e.reciprocal(out=o, in_=i_), [K("sc12")], [K("sc13")])
                ts("dve", sc[:, 1:2], sc[:, 13:14], -1.0, ALU.mult, [K("sc13")], [K("sc_beta")], s2=1.0, op1=ALU.add)
                if stop == "s_s1":
                    continue
                act(sc[:, 12:13], pB[:, 257:258], AF.Exp, ["pB", K("sc_r"), K("sc13"), "vec"], [K("sc12")], scale=sc[:, 0:1], bias=dtb_b[:, h:h + 1])
                act(sc[:, 13:14], sc[:, 12:13], AF.Ln, [K("sc12"), K("sc_beta"), "cst"], [K("sc13")], bias=cst[:, 2:3])
                ts("dve", sc[:, 2:3], sc[:, 13:14], nega[h], ALU.mult, [K("sc13"), "sm_a"], [K("sc_g")])
                if stop == "s_s2":
                    continue
                gU, eR, GT, DT, DTs = gU_[p], eR_[p], GT_[p], DT_[p], DTs_[p]
                act(gU[:], Uf, AF.Identity, ["cm", K("sc_g")], [K("gU")], scale=sc[:, 2:3])
                mm(pS[:, 384:512], onesf[:], gU[:], ["onesf", K("gU")], ["pS"])
                tt("dve", GT[:], pS[:, 384:512], idf, ALU.mult, ["pS", "cm"], [K("GT")])
                red(sc[:, 3:4], GT[:], ALU.add, [K("GT")], [K("sc_gc")])
                if stop == "s_s3":
                    continue
                cp("dve", sc[:, 6:7], pS[:, 511:512], ["pS"], [K("sc_gl")])
                act(eR[:], pS[:, 384:512], AF.Exp, ["pS"], [K("eR")])
                stt(GT[:], pS[:, 384:512], sc[:, 3:4], negm, ALU.subtract, ALU.add, ["pS", K("sc_gc"), "cm"], [K("GT")])
                act(DT[:], GT[:], AF.Exp, [K("GT")], [K("DT")])
                tt("pool", DTs[:], DT[:], strictm, ALU.mult, [K("DT"), "cm"], [K("DTs")])
                if stop == "s_s4":
                    continue
                act(sc[:, 12:13], sc[:, 3:4], AF.Exp, [K("sc_gc"), K("sc_g")], [K("sc12")])
                tt("dve", sc[:, 4:5], sc[:, 12:13], sc[:, 1:2], ALU.mult, [K("sc12"), K("sc_beta")], [K("sc_kbg")])
                act(sc[:, 5:6], sc[:, 3:4], AF.Exp, [K("sc_gc"), K("sc_gl")], [K("sc_kdec")], scale=-1.0, bias=sc[:, 6:7])
                act(sc[:, 7:8], sc[:, 6:7], AF.Exp, [K("sc_gl")], [K("sc_dec")])
                if stop == "s_scal":
                    continue
                kb, kbg, kdec, vb, kbT = kb_[p], kbg_[p], kdec_[p], vb_[p], kbT_[p]
                act(kbg[:], ktok[:], AF.Identity, [K("ktok"), K("sc_kbg")], [K("kbg")], scale=sc[:, 4:5])
                ts("dve", kdec[:], ktok[:], sc[:, 5:6], ALU.mult, [K("ktok"), K("sc_kdec")], [K("kdec")])
                act(vb[:], vtok[:], AF.Identity, [K("vtok"), K("sc_beta")], [K("vb")], scale=sc[:, 1:2])
                act(kb[:], idb[:], AF.Identity, ["idb", K("sc_beta")], [K("kb")], scale=sc[:, 1:2])
                mm(pS[:, 0:128], onesb[:], kb[:], ["onesb", K("kb")], ["pS"])
                tt("dve", kbT[:], knT[:], pS[:, 0:128], ALU.mult, [K("knT"), "pS"], [K("kbT")])
                qgT, AqkT = qgT_[p], AqkT_[p]
                tt("pool", qgT[:], qnT[:], eR[:], ALU.mult, [K("qnT"), K("eR")], [K("qgT")])
                if stop == "s_kvar":
                    continue
                mm(pG[:, 0:128], knT[:], kbT[:], [K("knT"), K("kbT")], ["pG"])
                mm(pG[:, 128:256], knT[:], qnT[:], [K("knT"), K("qnT")], ["pG"])
                q4 = b % 4
                nA, nB, PT = nA_[q4], nB_[q4], PT_[q4]
                K4 = lambda n: "%s%d" % (n, q4)
                stt(nB[:], pG[:, 0:128], -1.0, DTs[:], ALU.mult, ALU.mult, ["pG", K("DTs")], [K4("nB")])
                tt("dve", AqkT[:], pG[:, 128:256], DT[:], ALU.mult, ["pG", K("DT")], [K("AqkT")])
                mm(pG[:, 256:384], kbT[:], knT[:], [K("knT"), K("kbT")], ["pG"])
                G2, Ds = ksq_[p], kq_[p]
                stt(G2[:], pS[:, 384:512], sc[:, 3:4], posm, ALU.subtract, ALU.add, ["pS", K("sc_gc"), "cm"], [K("ksq")])
                act(Ds[:], G2[:], AF.Exp, [K("ksq")], [K("kq")], scale=-1.0)
                stt(nA[:], pG[:, 256:384], -1.0, Ds[:], ALU.mult, ALU.mult, ["pG", K("kq")], [K4("nA")])
                tt("pool", PT[:], nB[:], idb[:], ALU.add, [K4("nB"), "idb"], [K4("PT")])
                pNb, pNk = (pN, "pN") if b % 2 == 0 else (pM, "pM")
                for lvl in range(7):
                    if lvl >= 1:
                        mm(pNb[:, 256:384], nA[:], PT[:], [K4("nA"), K4("PT")], [pNk])
                    if lvl <= 4:
                        mm(pNb[:, 0:128], nA[:], nB[:], [K4("nA"), K4("nB")], [pNk])
                    if lvl <= 5:
                        mm(pNb[:, 128:256], nB[:], nA[:], [K4("nA"), K4("nB")], [pNk])
                    if lvl >= 1:
                        tt("dve", PT[:], pNb[:, 256:384], PT[:], ALU.add, [pNk, K4("PT")], [K4("PT")])
                    if lvl <= 4:
                        cp("act", nAB_[q4][:, 0:256], pNb[:, 0:256], [pNk], [K4("nB"), K4("nA")])
                    elif lvl == 5:
                        cp("act", nA[:], pNb[:, 128:256], [pNk], [K4("nA")])
                if stop == "s_neu":
                    continue
                u, wT, vnew = u_[p], wT_[p], vnew_[p]
                mm(pNb[:, 384:512], PT[:], vb[:], [K4("PT"), K("vb")], [pNk])
                mm(pNb[:, 0:128], kbg[:], PT[:], [K4("PT"), K("kbg")], [pNk])
                cp("act", u[:], pNb[:, 384:512], [pNk], [K("u")])
                cp("act", wT[:], pNb[:, 0:128], [pNk], [K("wT")])
                if stop == "s_uw":
                    continue
                mm(pQ[:, 0:128], wT[:], Sb_[:], [K("wT"), "Sb"], ["pQ"])
                tt("dve", vnew[:], u[:], pQ[:, 0:128], ALU.subtract, [K("u"), "pQ"], [K("vnew")])
                mm(pQ[:, 128:256], qgT[:], Sb_[:], [K("qgT"), "Sb"], ["pQ"], start=True, stop=False)
                mm(pQ[:, 128:256], AqkT[:], vnew[:], [K("AqkT"), K("vnew")], ["pQ"], start=False, stop=True)
                mm(pQ[:, 256:384], kdec[:], vnew[:], [K("kdec"), K("vnew")], ["pQ"])
                stt(Sf[:], Sf[:], sc[:, 7:8], pQ[:, 256:384], ALU.mult, ALU.add, ["Sf", K("sc_dec"), "pQ"], ["Sf"])
                cp("act", Sb_[:], Sf[:], ["Sf"], ["Sb"])
                j, m = b // 8, b % 8
                if m == 0:
                    ts("dve", oacc[:, j, :], pQ[:, 128:256], sel[:, 0:1], ALU.mult, ["pQ", "sel"], ["oacc%d" % j])
                else:
                    stt(oacc[:, j, :], pQ[:, 128:256], sel[:, m:m + 1], oacc[:, j, :], ALU.mult, ALU.add, ["pQ", "sel", "oacc%d" % j], ["oacc%d" % j])
                if stop == "s_seq":
                    continue
            if h == 0:
                dbg_dump("KT", KT[:], [128, S_], ["KT%d" % b for b in range(NBLK)], BF16)
                dbg_dump("oacc", oacc[:], [128, NOWN, 128], ["oacc%d" % j for j in range(NOWN)])

            if stop is not None and stop.startswith("s"):
                break
            for i in range(NOWN):
                p = i % 2
                K = lambda n: "%s%d" % (n, p)
                sc, lnt = sc_[p], lnt_[p]
                tt("pool", obn[p][:], oacc[:, i, :], oacc[:, i, :], ALU.mult, ["oacc%d" % i], [K("obn")])
                red(sc[:, 8:9], obn[p][:], ALU.add, [K("obn")], [K("sc_ms")])
                rsqrt_act(sc[:, 10:11], sc[:, 8:9], [K("sc_ms")], [K("sc_rn")], lnt[:, 0:1], K("lnt"), 1.0 / 128, cst[:, 0:1])
                stt(obn[p][:], oacc[:, i, :], sc[:, 10:11], gnw_b, ALU.mult, ALU.mult, ["oacc%d" % i, K("sc_rn"), "vec", K("sc_ms")], [K("obn")])
                tt("dve", obb[p][:], obn[p][:], zas[:, i, :], ALU.mult, [K("obn"), "zas%d" % i], [K("obb")])
                tr(pT[:, 640:768], obb[p][:], idb[:], [K("obb"), "idb"], ["pT"])
                cp("act", oTs[p][:], pT[:, 640:768], ["pT"], [K("oTs")])
                S.dma(oscr[:, h, i * 128:(i + 1) * 128], oTs[p][:], reads=[K("oTs")], writes=["oscr_%d_%d" % (h, i)], slot="oscw%d" % p)

            if stop == "gdnfin":
                break
            gi = 0
            for i in range(NOWN):
                p = i % 2
                K = lambda n: "%s%d" % (n, p)
                sc, lnt = sc_[p], lnt_[p]
                nkb = 8 * (i + 1)
                for g0 in range(0, nkb, 2):
                    sp_, spk = (pA, "pA") if gi % 2 == 0 else (pB, "pB")
                    pt, ptk = Pt[gi % 3], "Pt%d" % (gi % 3)
                    gi += 1
                    for m in range(2):
                        kb_i = g0 + m
                        mm(sp_[:, m * 256:(m + 1) * 256], KT[:, kb_i * 128:(kb_i + 1) * 128], QT2[:, i, :],
                           ["KT%d" % kb_i, "QT%d" % i, "QT2z"], [spk])
                    act(pt[:], sp_[:], AF.Exp, [spk, "sm_bnd"], [ptk], scale=0.125, bias=nbnd_ap)
                    if g0 >= nkb - 8:
                        mo = g0 - (nkb - 8)
                        ptv = pt[:].rearrange("p (m c q) -> p m c q", m=2, c=2)
                        tt("dve", ptv, ptv, amb[:, mo:mo + 2, :].unsqueeze(2).broadcast_to([128, 2, 2, 128]), ALU.mult, [ptk, "amb"], [ptk])
                    for m in range(2):
                        kb_i = g0 + m
                        for c in range(2):
                            accp, acck = (pG, "pG") if c == 0 else (pN, "pN")
                            mm(accp[:, 0:129], pt[:, m * 256 + c * 128:m * 256 + (c + 1) * 128], Vg[:, kb_i, 0:129],
                               [ptk, "Vg%d" % kb_i, "Vg_ones"], [acck], start=(kb_i == 0), stop=(kb_i == nkb - 1))
                S.op("dve", lambda e, o=sc[:, 12:13], i_=pG[:, 128:129]: e.reciprocal(out=o, in_=i_), ["pG"], [K("sc12")])
                S.op("dve", lambda e, o=sc[:, 13:14], i_=pN[:, 128:129]: e.reciprocal(out=o, in_=i_), ["pN"], [K("sc13")])
                tt("dve", sc[:, 13:14], sc[:, 13:14], nlam_ap, ALU.mult, [K("sc13"), "sm_nlam"], [K("sc13")])
                ts("dve", t0[p][:], pG[:, 0:128], sc[:, 12:13], ALU.mult, ["pG", K("sc12")], [K("t0")])
                stt(ob[p][:], pN[:, 0:128], sc[:, 13:14], t0[p][:], ALU.mult, ALU.add, ["pN", K("sc13"), K("t0")], [K("ob")])
                tt("pool", obn[p][:], ob[p][:], ob[p][:], ALU.mult, [K("ob")], [K("obn")])
                red(sc[:, 8:9], obn[p][:], ALU.add, [K("obn")], [K("sc_ms")])
                rsqrt_act(sc[:, 10:11], sc[:, 8:9], [K("sc_ms")], [K("sc_rn")], lnt[:, 0:1], K("lnt"), 1.0 / 128, cst[:, 1:2])
                ts("dve", sc[:, 10:11], sc[:, 10:11], 1.0 - LAM_INIT, ALU.mult, [K("sc_rn")], [K("sc_rn")])
                stt(obn[p][:], ob[p][:], sc[:, 10:11], snw_b, ALU.mult, ALU.mult, [K("ob"), K("sc_rn"), "vec", K("sc_ms")], [K("obn")])
                tt("dve", obb[p][:], obn[p][:], zbs[:, i, :], ALU.mult, [K("obn"), "zbs%d" % i], [K("obb")])
                tr(pT[:, 640:768], obb[p][:], idb[:], [K("obb"), "idb"], ["pT"])
                cp("act", oTs[p][:], pT[:, 640:768], ["pT"], [K("oTs")])
                S.dma(oscr[:, 4 + h, i * 128:(i + 1) * 128], oTs[p][:], reads=[K("oTs")], writes=["oscr_%d_%d" % (4 + h, i)], slot="oscw%d" % p)

        wo_v = w_out.rearrange("(c p) n -> p c n", p=128)
        for gi_ in range(8):
            st, sk = Wst[0], "Wst0"
            S.dma(st[:, :, :], wo_v[:, :, gi_ * 128:(gi_ + 1) * 128], writes=[sk], slot=sk)
            cp("dve" if gi_ % 2 == 0 else "pool", Wb[:, :, gi_ * 128:(gi_ + 1) * 128], st[:, :, :], [sk], ["Wb"])
        oTl = [xb_[0], xb_[1]]
        xol = [xTt[i][:, 0:4, :].rearrange("p c t -> p (c t)") for i in range(2)]
        outl = [xTt[i][:, 4:8, :].rearrange("p c t -> p (c t)") for i in range(2)]
        fin = []
        if stop is not None:
            mset("pool", outl[0], 0.0, ["outl00", "outl10", "xTt0"])
            for i in range(NOWN):
                S.dma(outd[i * 128:(i + 1) * 128, :], outl[0], reads=["outl00", "outl10"], writes=["out%d" % i], slot="outw0")
                fin.append("out%d" % i)
        for i in range(NOWN if stop is None else 0):
            p = i % 2
            K = lambda n: "%s%d" % (n, p)
            S.dma(oTl[p][:], oscr[:, :, i * 128:(i + 1) * 128], reads=["oscr_%d_%d" % (ch, i) for ch in range(8)], writes=[K("xb")], slot="oTl%d" % p)
            S.dma(xol[p], xo[i * 128:(i + 1) * 128, :], writes=[K("xol"), "xTt%d" % p], slot="xol%d" % p)
            for half, (pp, ppk) in enumerate(((pA, "pA"), (pB, "pB"))):
                for ch in range(NCH):
                    mm(pp[:, :], oTl[p][:, ch, :], Wb[:, ch, half * 512:(half + 1) * 512], [K("xb"), "Wb"], [ppk], start=(ch == 0), stop=(ch == NCH - 1))
                tt("dve", outl[p][:, half * 512:(half + 1) * 512], pp[:, :], xol[p][:, half * 512:(half + 1) * 512], ALU.add,
                   [ppk, K("xol"), "xTt%d" % p], [K("outl%d" % half)])
            S.dma(outd[i * 128:(i + 1) * 128, :], outl[p], reads=[K("outl0"), K("outl1")], writes=["out%d" % i], slot="outw%d" % p)
            fin.append("out%d" % i)
        fin += ["dbgo_" + n for n in dbg_out]
        S.emit(final_keys=fin)
        nc._sched_stats = S.stats
    return nc


def host_inputs(inputs, NBLK):
    S_ = NBLK * 128
    x = np.asarray(inputs["x"], np.float32)[0, :S_]
    xTf = np.ascontiguousarray(x.T)
    g = lambda k: np.asarray(inputs[k], np.float32)[0]
    wnorm = np.ascontiguousarray(g("w_norm").reshape(NCH, 128).T)
    cwf = g("conv_w")
    convw = np.ascontiguousarray(cwf.reshape(4, 3, NH, 128).transpose(3, 1, 2, 0).reshape(128, 12, 4))
    vecs = np.concatenate([g("a_log"), g("dt_bias"), g("gdn_norm_w"), g("subln_w"), g("q_norm_w"), g("k_norm_w"),
                           g("lambda_q1"), g("lambda_k1"), g("lambda_q2"), g("lambda_k2")]).astype(np.float32)[None, :]
    inv_freq = (10000.0 ** (-np.arange(0, 64, 2, dtype=np.float32) / np.float32(64))).astype(np.float32)
    ang = np.arange(S_, dtype=np.float32)[:, None] * inv_freq[None, :]
    cos, sin = np.cos(ang).astype(np.float32), np.sin(ang).astype(np.float32)
    r = np.arange(128)
    U = (r[:, None] <= r[None, :]).astype(np.float32)
    cmask = np.stack([U, np.where(r[:, None] <= r[None, :], 0.0, NEG).astype(np.float32),
                      (r[:, None] < r[None, :]).astype(np.float32), np.eye(128, dtype=np.float32),
                      np.where(r[:, None] > r[None, :], 0.0, -NEG).astype(np.float32)], axis=1)
    w_in = np.ascontiguousarray(g("w_in"))
    w_out = np.ascontiguousarray(g("w_out"))
    maps = []
    for c in range(NCORE):
        blocks = np.arange(NBLK // NCORE) * 8 + c
        tok = (blocks[:, None] * 128 + r[None, :]).reshape(-1)
        am = np.zeros((128, 8, 128), np.float32)
        am[:, :c, :] = 1.0
        am[:, c, :] = U
        sel = np.zeros((128, 8), np.float32)
        sel[:, c] = 1.0
        maps.append(dict(xT=xTf, xTo=np.ascontiguousarray(xTf[:, tok]), xo=np.ascontiguousarray(x[tok]), w_in=w_in, w_out=w_out,
                         wnorm=wnorm, convw=convw, vecs=vecs, cosd=cos, sind=sin, coso=np.ascontiguousarray(cos[tok]),
                         sino=np.ascontiguousarray(sin[tok]), cmask=np.ascontiguousarray(cmask), amask=am, sel=sel))
    return maps


_NC_CACHE = {}


def run(inputs, NBLK, dbg=None, stop=None):
    key = (NBLK, tuple(sorted(dbg)) if dbg else None, stop)
    if key not in _NC_CACHE:
        _NC_CACHE[key] = build(NBLK, dbg, stop)
    nc = _NC_CACHE[key]
    maps = host_inputs(inputs, NBLK)
    res = run_bass_kernel_spmd(nc, maps, core_ids=list(range(NCORE)))
    S_ = NBLK * 128
    out = np.zeros((1, S_, D), np.float32)
    r = np.arange(128)
    for c in range(NCORE):
        blocks = np.arange(NBLK // NCORE) * 8 + c
        tok = (blocks[:, None] * 128 + r[None, :]).reshape(-1)
        out[0, tok] = res.results[c]["out"]
    return out, res


def kernel(**inputs):
    out, _ = run(inputs, 128)
    return out
```
